# Optimizing a Trainium2 kernel written in Bass

```python
import jax, jax.numpy as jnp
from jax import lax
import numpy as np

D_MODEL = 2048
BATCH = 8
SEQ = 2048
DEPTH = 1

D_MIX = D_MODEL
GM_WIDTH = D_MIX // 2
GM_GROUPS = 8
GM_DG = GM_WIDTH // GM_GROUPS
CHUNK = 128
SB_WIDTH = D_MIX - GM_WIDTH
SB_HEADS = 8
SB_HEAD_DIM = SB_WIDTH // SB_HEADS
Q_BLOCK = 128
D_FF = -(-(8 * D_MODEL) // (3 * 256)) * 256
N_IN = 2 * GM_WIDTH + 3 * SB_WIDTH
N_MOD = 6
EPS = 1e-6

kernel_name = "hybrid_gmlp_stickbreaking_adaln_block"


def rmsnorm(x, g):
    xf = x.astype(jnp.float32)
    y = xf * lax.rsqrt(jnp.mean(xf * xf, axis=-1, keepdims=True) + EPS)
    return (y * g.astype(jnp.float32)).astype(x.dtype)


def group_rmsnorm(x, g, groups):
    xf = x.astype(jnp.float32).reshape(*x.shape[:-1], groups, -1)
    y = xf * lax.rsqrt(jnp.mean(xf * xf, axis=-1, keepdims=True) + EPS)
    return (y.reshape(x.shape) * g.astype(jnp.float32)).astype(x.dtype)


def group_layernorm(x, g, groups):
    xf = x.astype(jnp.float32).reshape(*x.shape[:-1], groups, -1)
    mu = jnp.mean(xf, axis=-1, keepdims=True)
    xc = xf - mu
    y = xc * lax.rsqrt(jnp.mean(xc * xc, axis=-1, keepdims=True) + EPS)
    return (y.reshape(x.shape) * g.astype(jnp.float32)).astype(x.dtype)


def chunked_spatial_gating(z, v_norm_g, w_s, b_s):
    B, S, _ = z.shape
    u, v = z[..., :GM_WIDTH], z[..., GM_WIDTH:]
    v = group_layernorm(v, v_norm_g, GM_GROUPS)
    v = v.reshape(B, S // CHUNK, CHUNK, GM_GROUPS, GM_DG)
    causal = jnp.tril(jnp.ones((CHUNK, CHUNK), dtype=bool))
    w = jnp.where(causal[None], w_s, 0).astype(v.dtype)
    mixed = jnp.einsum('gts,bnsgd->bntgd', w, v) + b_s.T.astype(v.dtype)[None, None, :, :, None]
    return u * mixed.reshape(B, S, GM_WIDTH)


def stick_breaking_attention(q, k, v):
    B, S, H, Dh = q.shape
    scale = Dh ** -0.5
    outs = []
    for i in range(S // Q_BLOCK):
        start, end = i * Q_BLOCK, (i + 1) * Q_BLOCK
        qs, ks, vs = q[:, start:end], k[:, :end], v[:, :end]
        z = jnp.einsum('bqhd,bkhd->bhqk', qs, ks).astype(jnp.float32) * scale
        t_pos = start + jnp.arange(Q_BLOCK)[:, None]
        s_pos = jnp.arange(end)[None, :]
        mask = s_pos < t_pos
        log_beta = jax.nn.log_sigmoid(z)
        log_1m = jnp.where(mask, jax.nn.log_sigmoid(-z), 0.0)
        tail = lax.cumsum(log_1m, axis=3, reverse=True) - log_1m
        a = jnp.where(mask, jnp.exp(log_beta + tail), 0.0)
        outs.append(jnp.einsum('bhqk,bkhd->bqhd', a.astype(vs.dtype), vs))
    return jnp.concatenate(outs, axis=1)


def setup_inputs(seed: int = 0) -> dict:
    key = jax.random.key(seed)
    ks = jax.random.split(key, 16)
    f32 = jnp.float32
    n = lambda k, shape: jax.random.normal(k, shape, dtype=f32)
    return {
        "x": n(ks[0], (BATCH, SEQ, D_MODEL)),
        "c": n(ks[1], (BATCH, D_MODEL)),
        "w_ada": n(ks[2], (DEPTH, D_MODEL, N_MOD * D_MODEL)) * (0.5 * D_MODEL ** -0.5),
        "b_ada": n(ks[3], (DEPTH, N_MOD * D_MODEL)) * 0.01,
        "norm1_g": 1.0 + 0.01 * n(ks[4], (DEPTH, D_MODEL)),
        "w_in": n(ks[5], (DEPTH, D_MODEL, N_IN)) * D_MODEL ** -0.5,
        "v_norm_g": 1.0 + 0.01 * n(ks[6], (DEPTH, GM_WIDTH)),
        "w_spatial": n(ks[7], (DEPTH, GM_GROUPS, CHUNK, CHUNK)) * CHUNK ** -0.5,
        "b_spatial": 1.0 + 0.01 * n(ks[8], (DEPTH, GM_GROUPS, CHUNK)),
        "out_norm_g": 1.0 + 0.01 * n(ks[9], (DEPTH, D_MIX)),
        "w_out": n(ks[10], (DEPTH, D_MIX, D_MODEL)) * D_MIX ** -0.5,
        "norm2_g": 1.0 + 0.01 * n(ks[11], (DEPTH, D_MODEL)),
        "w_gate": n(ks[12], (DEPTH, D_MODEL, D_FF)) * D_MODEL ** -0.5,
        "w_up": n(ks[13], (DEPTH, D_MODEL, D_FF)) * D_MODEL ** -0.5,
        "w_down": n(ks[14], (DEPTH, D_FF, D_MODEL)) * D_FF ** -0.5,
        "final_g": 1.0 + 0.01 * n(ks[15], (D_MODEL,)),
    }


def reference(x, c, w_ada, b_ada, norm1_g, w_in, v_norm_g, w_spatial, b_spatial,
              out_norm_g, w_out, norm2_g, w_gate, w_up, w_down, final_g):
    B, S, _ = x.shape
    c_act = jax.nn.silu(c)
    for l in range(DEPTH):
        mod = c_act @ w_ada[l] + b_ada[l]
        shift1, scale1, gate1, shift2, scale2, gate2 = [m[:, None, :] for m in jnp.split(mod, N_MOD, axis=-1)]

        h = rmsnorm(x, norm1_g[l]) * (1.0 + scale1) + shift1
        proj = h @ w_in[l]
        z_gm = jax.nn.gelu(proj[..., :2 * GM_WIDTH], approximate=False)
        o_gm = chunked_spatial_gating(z_gm, v_norm_g[l], w_spatial[l], b_spatial[l])
        qkv = proj[..., 2 * GM_WIDTH:].reshape(B, S, 3, SB_HEADS, SB_HEAD_DIM)
        o_sb = stick_breaking_attention(qkv[:, :, 0], qkv[:, :, 1], qkv[:, :, 2]).reshape(B, S, SB_WIDTH)
        o = jnp.concatenate([o_gm, o_sb], axis=-1)
        o = group_rmsnorm(o, out_norm_g[l], GM_GROUPS + SB_HEADS)
        x = x + gate1 * (o @ w_out[l])

        h = rmsnorm(x, norm2_g[l]) * (1.0 + scale2) + shift2
        f = (jax.nn.silu(h @ w_gate[l]) * (h @ w_up[l])) @ w_down[l]
        x = x + gate2 * f
    return rmsnorm(x, final_g)
```

```python
from contextlib import ExitStack

import numpy as np

import concourse.bass as bass
import concourse.mybir as mybir
from concourse.bass_utils import run_bass_kernel_spmd

F32 = mybir.dt.float32
BF16 = mybir.dt.bfloat16
AF = mybir.ActivationFunctionType
ALU = mybir.AluOpType

D = 2048
S = 2048
NCH = 16
DFF = 5632
NFF = 44
NMOD = 96
EPS = 1e-6
SCALE = 128.0 ** -0.5
TG = 512
NTG = S // TG

V_BADA, V_G1, V_G2, V_GF, V_GON, V_GV = 0, 96, 112, 128, 144, 160
NVEC = 168

ARENA_BYTES = 136 * 1024
REORDER = True


class Op:
    __slots__ = ("eng", "fn", "reads", "writes", "dma", "cost", "idx", "deps", "succs", "needs_signal", "sig",
                 "nwait", "fin")

    def __init__(self, eng, fn, reads, writes, dma, cost, idx):
        self.eng = eng
        self.fn = fn
        self.reads = tuple(reads)
        self.writes = tuple(writes)
        self.dma = dma
        self.cost = cost
        self.idx = idx
        self.deps = []
        self.succs = []
        self.needs_signal = dma
        self.sig = None
        self.nwait = 0
        self.fin = 0.0


DEFAULT_COST = {"pe": 0.3, "act": 0.65, "dve": 0.65, "pool": 1.0, "sp": 3.0}
SCHED_LAT = 0.2
SCHED_Q = 0.3
SCHED_LEVEL = True


class Prog:
    ENGS = ("pe", "act", "dve", "pool", "sp")

    def __init__(self, nc, es, n_dma_sems=40):
        self.nc = nc
        self.segs = [[]]
        self.nops = 0
        self.streams = {e: [] for e in self.ENGS}
        self.sems = {e: es.enter_context(nc.semaphore("s_" + e)) for e in self.ENGS}
        self.dma_sems = [es.enter_context(nc.semaphore("d%d" % i)) for i in range(n_dma_sems)]

    def op(self, eng, fn, reads=(), writes=(), dma=False, cost=None, name=""):
        if cost is None:
            cost = DEFAULT_COST["sp" if dma else eng]
        o = Op(eng, fn, reads, writes, dma, cost, self.nops)
        self.nops += 1
        self.segs[-1].append(o)
        return o

    def barrier(self):
        if self.segs[-1]:
            self.segs.append([])

    @staticmethod
    def _build_deps(seg):
        lastw = {}
        readers = {}
        for o in seg:
            deps = {}
            for k in o.reads:
                w = lastw.get(k)
                if w is not None:
                    deps[id(w)] = w
            for k in o.writes:
                w = lastw.get(k)
                if w is not None:
                    deps[id(w)] = w
                for r in readers.get(k, ()):
                    deps[id(r)] = r
            deps.pop(id(o), None)
            o.deps = list(deps.values())
            for d in o.deps:
                d.succs.append(o)
            for k in o.reads:
                readers.setdefault(k, []).append(o)
            for k in o.writes:
                lastw[k] = o
                readers[k] = []

    def _list_schedule(self, seg):
        level = {}
        for o in reversed(seg):
            m = 0.0
            for s_ in o.succs:
                lv = level[id(s_)]
                if lv > m:
                    m = lv
            level[id(o)] = m + o.cost + SCHED_LAT
        indeg = {id(o): len(o.deps) for o in seg}
        ready = {e: [] for e in self.ENGS}
        for o in seg:
            if not o.deps:
                ready[o.eng].append(o)
        t_free = {e: 0.0 for e in self.ENGS}
        order = []
        n = len(seg)
        while len(order) < n:
            best = None
            for e in self.ENGS:
                lst = ready[e]
                if not lst:
                    continue
                tf = t_free[e]
                for o in lst:
                    st = tf
                    for d in o.deps:
                        f = d.fin + SCHED_LAT
                        if f > st:
                            st = f
                    key = (int(st / SCHED_Q), -level[id(o)] if SCHED_LEVEL else 0.0, o.idx)
                    if best is None or key < best[0]:
                        best = (key, o, st)
            _, o, st = best
            e = o.eng
            ready[e].remove(o)
            o.fin = st + o.cost
            t_free[e] = st + (0.5 if o.dma else o.cost)
            order.append(o)
            for s_ in o.succs:
                indeg[id(s_)] -= 1
                if indeg[id(s_)] == 0:
                    ready[s_.eng].append(s_)
        return order, max(t_free.values())

    def schedule(self, reorder=True):
        prev_lasts = []
        last_compute = {}
        nsem = len(self.dma_sems)
        dma_last = [None] * nsem
        dma_cnt = [0] * nsem
        half = nsem // 2
        rr_q = {"sp": 0, "pool": 0}
        base_q = {"sp": 0, "pool": half}
        est_total = 0.0
        for seg in self.segs:
            self._build_deps(seg)
            if reorder:
                order, est = self._list_schedule(seg)
                est_total += est
            else:
                order = seg
            seen = set()
            seg_dmas = []
            for o in order:
                if prev_lasts and o.eng not in seen:
                    o.deps = o.deps + [d for d in prev_lasts if d is not o]
                seen.add(o.eng)
                if o.dma:
                    rr = base_q[o.eng] + rr_q[o.eng]
                    rr_q[o.eng] = (rr_q[o.eng] + 1) % half
                    prev = dma_last[rr]
                    if prev is not None:
                        o.deps.append(prev)
                    dma_cnt[rr] += 1
                    o.sig = (self.dma_sems[rr], 16 * dma_cnt[rr])
                    dma_last[rr] = o
                    seg_dmas.append(o)
                else:
                    last_compute[o.eng] = o
                self.streams[o.eng].append(o)
            prev_lasts = list(last_compute.values()) + seg_dmas
        for e in self.ENGS:
            for o in self.streams[e]:
                kept = []
                for d in o.deps:
                    if d.dma or o.dma or d.eng != o.eng or o.eng != "pe":
                        kept.append(d)
                        d.needs_signal = True
                o.deps = kept
        for e in self.ENGS:
            cnt = 0
            for o in self.streams[e]:
                if o.dma:
                    continue
                if o.needs_signal:
                    cnt += 1
                    o.sig = (self.sems[e], cnt)
        return est_total

    def replay(self, eng_name, eng):
        waited = {}
        for o in self.streams[eng_name]:
            for d in o.deps:
                sem, val = d.sig
                key = id(sem)
                if waited.get(key, 0) < val:
                    eng.wait_ge(sem, val)
                    waited[key] = val
            ins = o.fn(eng)
            if o.dma:
                ins.then_inc(o.sig[0], 16)
            elif o.needs_signal:
                ins.then_inc(o.sig[0], 1)
        return waited

    def final_wait(self, eng_name, eng, waited, ops):
        for d in ops:
            sem, val = d.sig
            if waited.get(id(sem), 0) < val:
                eng.wait_ge(sem, val)
                waited[id(sem)] = val


class Arena:
    def __init__(self, t, nbytes):
        self.t = t
        self.nbytes = nbytes
        self.top = 0
        self.hi = 0

    def mark(self):
        return self.top

    def release(self, m):
        self.top = m

    def alloc(self, nbytes, dtype=F32, shape=None):
        assert nbytes % 4 == 0
        off = self.top
        self.top += (nbytes + 63) // 64 * 64
        self.hi = max(self.hi, self.top)
        assert self.top <= self.nbytes, "arena overflow %d > %d" % (self.top, self.nbytes)
        v = self.t[:, off // 4:(off + nbytes) // 4]
        if dtype == BF16:
            v = v.bitcast(BF16)
        if shape is not None:
            names = " ".join("a%d" % i for i in range(len(shape)))
            kw = {"a%d" % i: int(s) for i, s in enumerate(shape)}
            v = v.rearrange("p (%s) -> p %s" % (names, names), **kw)
        return v


def roundrobin(gens, weights):
    gens = list(gens)
    alive = [True] * len(gens)
    while any(alive):
        for i, g in enumerate(gens):
            if not alive[i]:
                continue
            for _ in range(weights[i]):
                try:
                    next(g)
                except StopIteration:
                    alive[i] = False
                    break


def build_nc(upto=99, debug=False):
    nc = bass.Bass("TRN2", target_bir_lowering=False)
    dt_in = lambda name, shape: nc.dram_tensor(name, list(shape), F32, kind="ExternalInput").ap()
    xT = dt_in("xT", [D, S])
    cb = dt_in("cb", [1, D])
    wadaT = dt_in("wadaT", [NMOD * 128, D])
    vecs_d = dt_in("vecs", [128, NVEC])
    bsp_d = dt_in("bsp", [1, 1024])
    wsT_d = dt_in("wsT", [128, 1024])
    cmask_d = dt_in("cmask", [128, 512])
    w_in_r = dt_in("w_in_r", [40, 128, 2048])
    w_out_r = dt_in("w_out_r", [8, 128, 8 * 512])
    w_gate_r = dt_in("w_gate_r", [22, 128, 8 * 512])
    w_up_r = dt_in("w_up_r", [22, 128, 8 * 512])
    w_down_r = dt_in("w_down_r", [16, 128, 11 * 512])
    outT = nc.dram_tensor("outT", [D, S], F32, kind="ExternalOutput").ap()
    dbg = {}
    if debug:
        dbg["hT"] = nc.dram_tensor("dbg_hT", [128, NCH * S], F32, kind="ExternalOutput").ap()
        dbg["oT"] = nc.dram_tensor("dbg_oT", [128, NCH * S], F32, kind="ExternalOutput").ap()
        dbg["mod"] = nc.dram_tensor("dbg_mod", [128, NMOD], F32, kind="ExternalOutput").ap()

    xT_v = xT.rearrange("(k p) s -> p k s", p=128)
    outT_v = outT.rearrange("(k p) s -> p k s", p=128)

    with ExitStack() as es:
        P = Prog(nc, es)
        sb = lambda name, shape, dt: es.enter_context(nc.sbuf_tensor("sb_" + name, list(shape), dt))
        ps = es.enter_context(nc.psum_tensor("ps", [128, 8, 512], F32))
        cm = sb("cm", [128, 512], BF16)
        cmf = sb("cmf", [128, 128], F32)
        ones = sb("ones", [128, 128], BF16)
        onesf = sb("onesf", [128, 128], F32)
        vecs = sb("vecs", [128, NVEC], F32)
        modT = sb("modT", [128, NMOD], F32)
        gmod = sb("gmod", [128, 32], F32)
        mhalf = sb("mhalf", [128, 16], F32)
        st_r = sb("st_r", [128, 2, 4], F32)
        biasT = sb("biasT", [128, 40], F32)
        sh1b = sb("sh1b", [128, 16], BF16)
        st_rb = sb("st_rb", [128, 2, 512], F32)
        oT = sb("oT", [128, NCH, S], BF16)
        arena_t = sb("arena", [128, ARENA_BYTES // 4], F32)
        AR = Arena(arena_t, ARENA_BYTES)

        tri = cm[:, 0:128]
        ltri = cm[:, 128:256]
        maskd = cm[:, 256:384]
        ident = cm[:, 384:512]
        G1 = gmod[:, 0:16]
        G2 = gmod[:, 16:32]
        SH1, SC1, GT1, SH2, SC2, GT2 = (modT[:, 16 * i:16 * (i + 1)] for i in range(6))

        def bankv(b):
            return ps[:, b, :]

        B_ST, B_BC = 6, 7
        rs_cnt = [0]

        def rstd_p1(srcs, keys, n, nt=4):
            slot = rs_cnt[0] % 2
            rs_cnt[0] += 1
            r = st_r[:, slot, 0:nt]
            nsrc = len(srcs)

            def mm(e):
                for i in range(nt):
                    for c, s_ap in enumerate(srcs):
                        ins = e.matmul(ps[:, B_ST, i:i + 1], lhsT=s_ap[:, i * 128:(i + 1) * 128], rhs=ones[:, 0:1],
                                       start=(c == 0), stop=(c == nsrc - 1), skip_group_check=True)
                return ins
            P.op("pe", mm, reads=[("ones",)] + list(keys), writes=[("ps", B_ST)], cost=0.1 + 0.07 * nt * nsrc)
            P.op("dve", lambda e: e.tensor_scalar(out=r, in0=ps[:, B_ST, 0:nt], scalar1=1.0 / n, scalar2=EPS,
                                                  op0=ALU.mult, op1=ALU.add),
                 writes=[("ps", B_ST), ("st_r", slot)], cost=0.2)
            P.op("pool", lambda e: e.tensor_tensor(out=r, in0=r, in1=mhalf[:, 0:nt], op=ALU.pow),
                 reads=[("mhalf",)], writes=[("st_r", slot)], cost=1.1)
            return slot

        def rstd_p1b(slot, nt=4):
            r = st_r[:, slot, :]
            rb = st_rb[:, slot, :]

            def bc(e):
                for i in range(nt):
                    ins = e.activation(out=rb[:, i * 128:(i + 1) * 128], in_=onesf[:, :], func=AF.Copy, scale=r[:, i:i + 1])
                return ins
            P.op("act", bc, reads=[("st_r", slot), ("onesf",)], writes=[("st_rb", slot)], cost=0.38 * nt)

        def rstd_p2(slot, nt=4):
            rb = st_rb[:, slot, :]

            def mm2(e):
                for i in range(nt):
                    ins = e.matmul(ps[:, B_BC, i * 128:(i + 1) * 128], lhsT=rb[:, i * 128:(i + 1) * 128], rhs=cmf[:, :],
                                   start=True, stop=True, skip_group_check=True)
                return ins
            P.op("pe", mm2, reads=[("st_rb", slot), ("cm",)], writes=[("ps", B_BC)], cost=0.35 * nt)

        def rstd_bcast(srcs, keys, n, nt=4):
            slot = rstd_p1(srcs, keys, n, nt)
            rstd_p1b(slot, nt)
            rstd_p2(slot, nt)

        hT = AR.alloc(NCH * S * 2, BF16, (NCH, S))
        m_r1 = AR.mark()
        cact = AR.alloc(D * 4, F32)
        junk = AR.alloc(D * 2, BF16)
        NWA = 3
        wa = [AR.alloc(D * 4, F32) for _ in range(NWA)]
        m_r1b = AR.mark()

        P.op("pool", lambda e: e.dma_start(out=cm[:, :], in_=cmask_d), writes=[("cm",)], dma=True)
        P.op("sp", lambda e: e.dma_start(out=cmf[:, :], in_=cmask_d[:, 384:512]), writes=[("cm",)], dma=True)
        P.op("sp", lambda e: e.dma_start(out=vecs[:, :], in_=vecs_d), writes=[("vecs",)], dma=True)
        P.op("sp", lambda e: e.dma_start(out=cact[:, :], in_=cb.partition_broadcast(128)),
             writes=[("cact",)], dma=True)
        P.op("pool", lambda e: e.memset(ones[:, :], 1.0), writes=[("ones",)])
        P.op("pool", lambda e: e.memset(onesf[:, :], 1.0), writes=[("onesf",)])
        P.op("pool", lambda e: e.memset(mhalf[:, :], -0.5), writes=[("mhalf",)])
        P.op("act", lambda e: e.activation(out=cact[:, :], in_=cact[:, :], func=AF.Silu),
             reads=[("cact",)], writes=[("cact",)])

        def mod_block(j):
            slot = j % NWA
            P.op("sp", lambda e: e.dma_start(out=wa[slot][:, :], in_=wadaT[j * 128:(j + 1) * 128, :]),
                 writes=[("wa", slot)], dma=True)
            P.op("dve", lambda e: e.scalar_tensor_tensor(
                out=junk[:, :], in0=wa[slot][:, :], scalar=1.0, in1=cact[:, :],
                op0=ALU.mult, op1=ALU.mult, accum_out=modT[:, j:j + 1]),
                reads=[("wa", slot), ("cact",)], writes=[("junk",), ("modT", j // 16)], cost=2.35)

        def mod_finish(lo, hi):
            P.op("dve", lambda e: e.tensor_tensor(
                out=modT[:, lo * 16:hi * 16], in0=modT[:, lo * 16:hi * 16],
                in1=vecs[:, V_BADA + lo * 16:V_BADA + hi * 16], op=ALU.add),
                reads=[("modT", i) for i in range(lo, hi)] + [("vecs",)],
                writes=[("modT", i) for i in range(lo, hi)])

        def gmod_make(dst, sc_idx, gcol, key):
            P.op("dve", lambda e: e.scalar_tensor_tensor(
                out=dst, in0=modT[:, sc_idx * 16:(sc_idx + 1) * 16], scalar=1.0,
                in1=vecs[:, gcol:gcol + 16], op0=ALU.add, op1=ALU.mult),
                reads=[("modT", sc_idx), ("vecs",)], writes=[key])

        for j in range(32):
            mod_block(j)
        mod_finish(0, 2)
        gmod_make(G1, 1, V_G1, ("G1",))
        P.op("dve", lambda e: e.tensor_copy(out=sh1b[:, :], in_=SH1), reads=[("modT", 0)], writes=[("sh1b",)], cost=0.2)

        def mod2_thread():
            for j in range(32, NMOD):
                mod_block(j)
                yield
            mod_finish(2, 6)
            gmod_make(G2, 4, V_G2, ("G2",))
            yield

        HG = 256
        xt2 = [AR.alloc(NCH * HG * 4, F32, (NCH, HG)) for _ in range(2)]
        rsd = [AR.alloc(HG * 4, F32) for _ in range(2)]

        def p1a_load(q, hg, t0):
            xt = xt2[hg % 2]
            P.op("sp", lambda e: e.dma_start(out=xt[:, 4 * q:4 * q + 4, :],
                                             in_=xT_v[:, 4 * q:4 * q + 4, t0:t0 + HG]),
                 writes=[("xt", hg % 2, k) for k in range(4 * q, 4 * q + 4)], dma=True, cost=4.0)

        def p1a_square(k, hg, t0):
            xt = xt2[hg % 2]
            P.op("act", lambda e: e.activation(out=oT[:, k, t0:t0 + HG], in_=xt[:, k, :], func=AF.Square),
                 reads=[("xt", hg % 2, k)], writes=[("oT", k, hg)], cost=0.45)

        def p1a_rstd(hg, t0):
            rb_ = rsd[hg % 2]

            def mm(e):
                for k in range(NCH):
                    ins = e.matmul(ps[:, B_BC, 0:HG], lhsT=ones[:, :], rhs=oT[:, k, t0:t0 + HG],
                                   start=(k == 0), stop=(k == NCH - 1))
                return ins
            P.op("pe", mm, reads=[("ones",)] + [("oT", k, hg) for k in range(NCH)], writes=[("ps", B_BC)], cost=2.0)
            P.op("act", lambda e: e.activation(out=rb_[:, :], in_=ps[:, B_BC, 0:HG], func=AF.Sqrt, scale=1.0 / D, bias=EPS),
                 writes=[("ps", B_BC), ("rsd", hg % 2)], cost=0.45)
            P.op("dve", lambda e: e.reciprocal(out=rb_[:, :], in_=rb_[:, :]), reads=[("rsd", hg % 2)], writes=[("rsd", hg % 2)],
                 cost=0.4)

        def p1a_norm(k, hg, t0):
            xt = xt2[hg % 2]
            rb_ = rsd[hg % 2]
            P.op("dve", lambda e: e.scalar_tensor_tensor(out=hT[:, k, t0:t0 + HG], in0=xt[:, k, :], scalar=G1[:, k:k + 1],
                                                         in1=rb_[:, :], op0=ALU.mult, op1=ALU.mult),
                 reads=[("xt", hg % 2, k), ("G1",), ("rsd", hg % 2)], writes=[("hT", k, t0 // TG, (t0 % TG) // HG)], cost=0.42)

        def phase1a():
            for hg in range(S // HG):
                t0 = hg * HG
                for q in range(4):
                    p1a_load(q, hg, t0)
                for k in range(NCH):
                    p1a_square(k, hg, t0)
                p1a_rstd(hg, t0)
                for k in range(NCH):
                    p1a_norm(k, hg, t0)
                yield

        for _ in phase1a():
            pass

        out_dmas = []

        class WStream:
            def __init__(self, name, slots, loads, depth):
                self.name, self.slots, self.loads, self.depth = name, slots, loads, depth
                self.issued = 0

            post = None

            def _issue(self, i):
                slot = i % len(self.slots)
                dst = self.slots[slot]
                src = self.loads[i]
                n = src.shape[-1]
                name = self.name
                P.op("pool", lambda e: e.dma_start(out=dst[:, 0:n], in_=src), writes=[(name, slot)], dma=True, cost=2.0 + n * 128 * 4 / 330e3)
                if self.post is not None:
                    self.post(i, dst, (name, slot))

            def get(self, i):
                while self.issued < min(len(self.loads), i + self.depth + 1):
                    self._issue(self.issued)
                    self.issued += 1
                slot = i % len(self.slots)
                return self.slots[slot], (self.name, slot)

        def fold_adaln(blk, wslot, wkey):
            def mmb(e):
                for k in range(NCH):
                    ins = e.matmul(ps[:, B_ST, 8:9], lhsT=wslot[:, k * 128:(k + 1) * 128], rhs=sh1b[:, k:k + 1],
                                   start=(k == 0), stop=(k == NCH - 1), skip_group_check=True)
                return ins
            P.op("pe", mmb, reads=[wkey, ("sh1b",)], writes=[("ps", B_ST)], cost=1.2)
            P.op("dve", lambda e: e.tensor_copy(out=biasT[:, blk:blk + 1], in_=ps[:, B_ST, 8:9]),
                 writes=[("ps", B_ST), ("biasT", blk)], cost=0.15)


        def proj(wslot, wkey, tg, bank):
            t0 = tg * TG

            def mm(e):
                for k in range(NCH):
                    ins = e.matmul(bankv(bank), lhsT=wslot[:, k * 128:(k + 1) * 128], rhs=hT[:, k, t0:t0 + TG],
                                   start=(k == 0), stop=(k == NCH - 1))
                return ins
            P.op("pe", mm, reads=[wkey] + [("hT", k, tg, hh) for k in range(NCH) for hh in range(2)], writes=[("ps", bank)], cost=16 * 0.26)

        if upto >= 2:
            P.barrier()
            AR.release(m_r1b)
            NW1 = 3
            wslots1 = [AR.alloc(2048 * 2, BF16) for _ in range(NW1)]
            wsT = AR.alloc(1024 * 2, BF16, (8, 128))
            bspb = AR.alloc(1024 * 4, F32, (8, 128))
            u_sb = [AR.alloc(TG * 2, BF16) for _ in range(3)]
            v_sb = [AR.alloc(TG * 2, BF16) for _ in range(2)]
            vn = [AR.alloc(TG * 2, BF16, (4, 128)) for _ in range(2)]
            gtmp = [AR.alloc(TG * 4, F32)] * 2
            go = [AR.alloc(TG * 4, F32) for _ in range(3)]
            gsq = [AR.alloc(TG * 2, BF16) for _ in range(2)]
            bst = [AR.alloc(4 * 6 * 4, F32, (4, 6)) for _ in range(2)]
            bmv = [AR.alloc(4 * 2 * 4, F32, (4, 2)) for _ in range(2)]
            brs = [AR.alloc(4 * 4, F32) for _ in range(2)]
            bnb = [AR.alloc(4 * 4, F32) for _ in range(2)]

            P.op("pool", lambda e: e.dma_start(out=wsT.rearrange("p a b -> p (a b)"), in_=wsT_d), writes=[("wsT",)], dma=True)
            P.op("sp", lambda e: e.dma_start(out=bspb.rearrange("p a b -> p (a b)"), in_=bsp_d.partition_broadcast(128)),
                 writes=[("bspb",)], dma=True)

            def ws_mask(g):
                P.op("pool", lambda e: e.tensor_tensor(out=wsT[:, g, :], in0=wsT[:, g, :], in1=ltri, op=ALU.mult),
                     reads=[("wsT",), ("cm",)], writes=[("wsT",)])
            for g in range(8):
                ws_mask(g)

            ws1 = WStream("w1", wslots1, [w_in_r[i] for i in range(16)], depth=1)
            ws1.post = lambda i, dst, key: fold_adaln(i, dst, key)
            B_TB, B_MX = 4, 5
            units = [(g, tg) for g in range(8) for tg in range(NTG)]
            tbv1 = ps[:, B_TB, 0:256].bitcast(BF16)
            rslots = {}

            def S_PU(n):
                g, tg = units[n]
                wslot, wkey = ws1.get(2 * g)
                bu = n % 2
                ub = u_sb[n % 3]
                proj(wslot, wkey, tg, bu)
                P.op("act", lambda e: e.activation(out=ub[:, :], in_=bankv(bu), func=AF.Gelu, bias=biasT[:, 2 * g:2 * g + 1]),
                     reads=[("biasT", 2 * g)], writes=[("ps", bu), ("u_sb", n % 3)])

            def S_PV(n):
                g, tg = units[n]
                wslot, wkey = ws1.get(2 * g + 1)
                bv = 2 + n % 2
                vb = v_sb[n % 2]
                proj(wslot, wkey, tg, bv)
                P.op("act", lambda e: e.activation(out=vb[:, :], in_=bankv(bv), func=AF.Gelu, bias=biasT[:, 2 * g + 1:2 * g + 2]),
                     reads=[("biasT", 2 * g + 1)], writes=[("ps", bv), ("v_sb", n % 2)])

            def S_TR(n):
                par = n % 2
                vb, st6, mv, rs = v_sb[par], bst[par], bmv[par], brs[par]

                def tr(e):
                    for i in range(4):
                        ins = e.transpose(tbv1[:, i * 128:(i + 1) * 128], vb[:, i * 128:(i + 1) * 128], ident)
                    return ins
                P.op("pe", tr, reads=[("v_sb", par), ("cm",)], writes=[("ps", B_TB)], cost=0.45)

                def stats(e):
                    for i in range(4):
                        ins = e.bn_stats(st6[:, i, :], tbv1[:, i * 128:(i + 1) * 128])
                    return ins
                P.op("dve", stats, writes=[("ps", B_TB), ("bst", par)], cost=0.9)

                def aggr(e):
                    for i in range(4):
                        ins = e.bn_aggr(mv[:, i, :], st6[:, i, :])
                    return ins
                P.op("dve", aggr, reads=[("bst", par)], writes=[("bmv", par)], cost=0.4)
                P.op("dve", lambda e: e.tensor_scalar(out=rs[:, :], in0=mv[:, :, 1], scalar1=EPS, scalar2=None, op0=ALU.add),
                     reads=[("bmv", par)], writes=[("brs", par)])
                P.op("pool", lambda e: e.tensor_tensor(out=rs[:, :], in0=rs[:, :], in1=mhalf[:, 0:4], op=ALU.pow),
                     reads=[("mhalf",)], writes=[("brs", par)], cost=1.1)

            def S_VN(n):
                g, tg = units[n]
                par = n % 2
                vnb, mv, rs = vn[par], bmv[par], brs[par]

                nb = bnb[par]
                P.op("dve", lambda e: e.scalar_tensor_tensor(out=nb[:, :], in0=mv[:, :, 0], scalar=-1.0, in1=rs[:, :],
                                                             op0=ALU.mult, op1=ALU.mult),
                     reads=[("brs", par), ("bmv", par)], writes=[("bnb", par)])

                def vnorm(e):
                    for i in range(4):
                        ins = e.activation(out=vnb[:, i, :], in_=tbv1[:, i * 128:(i + 1) * 128], func=AF.Identity,
                                           scale=rs[:, i:i + 1], bias=nb[:, i:i + 1])
                    return ins
                P.op("act", vnorm, reads=[("brs", par), ("bnb", par)], writes=[("ps", B_TB), ("vn", par)], cost=1.5)

                def mix(e):
                    for i in range(4):
                        ins = e.matmul(ps[:, B_MX, i * 128:(i + 1) * 128], lhsT=vnb[:, i, :], rhs=wsT[:, g, :],
                                       start=True, stop=True, skip_group_check=True)
                    return ins
                P.op("pe", mix, reads=[("vn", par), ("wsT",)], writes=[("ps", B_MX)], cost=0.5)

            def S_EP(n):
                g, tg = units[n]
                par = n % 2
                tb, ob, sqb, ub = gtmp[par], go[n % 3], gsq[par], u_sb[n % 3]

                def ep1(e):
                    for i in range(4):
                        ins = e.scalar_tensor_tensor(out=tb[:, i * 128:(i + 1) * 128], in0=ps[:, B_MX, i * 128:(i + 1) * 128],
                                                     scalar=vecs[:, V_GV + g:V_GV + g + 1], in1=bspb[:, g, :],
                                                     op0=ALU.mult, op1=ALU.add)
                    return ins
                P.op("dve", ep1, reads=[("vecs",), ("bspb",)], writes=[("ps", B_MX), ("gtmp", 0)], cost=1.2)
                P.op("dve", lambda e: e.tensor_tensor(out=ob[:, :], in0=tb[:, :], in1=ub[:, :], op=ALU.mult),
                     reads=[("gtmp", 0), ("u_sb", n % 3)], writes=[("go", n % 3)], cost=0.7)
                P.op("act", lambda e: e.activation(out=sqb[:, :], in_=ob[:, :], func=AF.Square),
                     reads=[("go", n % 3)], writes=[("gsq", par)])

            def S_STAT(n):
                par = n % 2
                rslots[n] = rstd_p1([gsq[par]], [("gsq", par)], 128.0)

            def S_BC(n):
                rstd_p1b(rslots[n])
                rstd_p2(rslots[n])

            def S_FIN(n):
                g, tg = units[n]
                t0 = tg * TG
                ob = go[n % 3]
                P.op("dve", lambda e: e.scalar_tensor_tensor(
                    out=oT[:, g, t0:t0 + TG], in0=ob[:, :], scalar=vecs[:, V_GON + g:V_GON + g + 1], in1=bankv(B_BC),
                    op0=ALU.mult, op1=ALU.mult),
                    reads=[("go", n % 3), ("vecs",)], writes=[("ps", B_BC), ("oT", g, tg)])

            def phase1b():
                N = len(units)
                stages = [S_PU, S_PV, S_TR, S_VN, S_EP, S_STAT, S_BC, S_FIN]
                for hs in range(2 * N + 8):
                    for k, fn in enumerate(stages):
                        d = hs - k
                        if d >= 0 and d % 2 == 0 and d // 2 < N:
                            fn(d // 2)
                    yield

            roundrobin([phase1b(), mod2_thread()], [1, 1])

        if upto >= 3:
            P.barrier()
            AR.release(m_r1)
            NW2 = 3
            wslots2 = [AR.alloc(2048 * 2, BF16) for _ in range(NW2)]
            qT = [AR.alloc(S * 2, BF16) for _ in range(2)]
            kT = [AR.alloc(S * 2, BF16) for _ in range(2)]
            vtok = [AR.alloc(S * 2, BF16, (16, 128)) for _ in range(2)]
            vst = [AR.alloc(TG * 2, BF16) for _ in range(2)]
            NE, NSP, NPM, NTT, NAT = 2, 4, 4, 2, 3
            e_b = [AR.alloc(TG * 4, F32) for _ in range(NE)]
            sp_b = [AR.alloc(TG * 4, F32) for _ in range(NSP)]
            pm_b = [AR.alloc(TG * 2, BF16) for _ in range(NPM)]
            tt_b = [AR.alloc(TG * 4, F32) for _ in range(NTT)]
            at_b = [AR.alloc(TG * 2, BF16) for _ in range(NAT)]
            osb = [AR.alloc(TG * 4, F32) for _ in range(3)]
            asq = [AR.alloc(TG * 2, BF16) for _ in range(3)]
            ws2 = WStream("w2", wslots2, [w_in_r[16 + i] for i in range(24)], depth=0)
            ws2.post = lambda i, dst, key: fold_adaln(16 + i, dst, key)
            B_Z, B_ACC, B_AV, B_PJ, B_VT = (0, 1), 2, 3, (4, 5), 6

            pj_cnt = [0]
            NE, NSP, NPM, NTT, NAT = 2, 4, 4, 2, 3

            def proj_part(wslot, wkey, tg, bank, k0, k1):
                t0 = tg * TG

                def mm(e):
                    for k in range(k0, k1):
                        ins = e.matmul(bankv(bank), lhsT=wslot[:, k * 128:(k + 1) * 128], rhs=hT[:, k, t0:t0 + TG],
                                       start=(k == 0), stop=(k == NCH - 1))
                    return ins
                P.op("pe", mm, reads=[wkey] + [("hT", k, tg, hh) for k in range(k0, k1) for hh in range(2)], writes=[("ps", bank)], cost=(k1 - k0) * 0.26)

            def qkv_evac(h, which, tg, bank):
                hp = h % 2
                t0 = tg * TG
                blk = 16 + 3 * h + which
                bcol = biasT[:, blk:blk + 1]
                if which == 0:
                    P.op("dve", lambda e: e.tensor_scalar(out=qT[hp][:, t0:t0 + TG], in0=bankv(bank), scalar1=bcol, scalar2=None, op0=ALU.add),
                         reads=[("biasT", blk)], writes=[("ps", bank), ("qT", hp, tg)])
                elif which == 1:
                    P.op("dve", lambda e: e.tensor_scalar(out=kT[hp][:, t0:t0 + TG], in0=bankv(bank), scalar1=bcol, scalar2=None, op0=ALU.add),
                         reads=[("biasT", blk)], writes=[("ps", bank), ("kT", hp, tg)])
                else:
                    sp_ = tg % 2
                    vs = vst[sp_]
                    P.op("dve", lambda e: e.tensor_scalar(out=vs[:, :], in0=bankv(bank), scalar1=bcol, scalar2=None, op0=ALU.add),
                         reads=[("biasT", blk)], writes=[("ps", bank), ("vst", sp_)])

            def v_transpose(h, tg):
                hp = h % 2
                sp_ = tg % 2
                vs = vst[sp_]
                tbv = ps[:, B_VT, 256:512].bitcast(BF16)

                def tr(e):
                    for i in range(4):
                        ins = e.transpose(tbv[:, i * 128:(i + 1) * 128], vs[:, i * 128:(i + 1) * 128], ident)
                    return ins
                P.op("pe", tr, reads=[("vst", sp_), ("cm",)], writes=[("ps", B_VT)], cost=0.45)

            def v_evac(h, tg):
                hp = h % 2
                tbv = ps[:, B_VT, 256:512].bitcast(BF16)
                P.op("dve", lambda e: e.tensor_copy(out=vtok[hp][:, 4 * tg:4 * tg + 4, :],
                                                    in_=tbv.rearrange("p (a b) -> p a b", a=4)),
                     writes=[("ps", B_VT), ("vtok", hp, tg)])

            def qkv_thread(h):
                for tg in range(NTG):
                    for which in range(3):
                        wslot, wkey = ws2.get(3 * h + which)
                        bank = B_PJ[pj_cnt[0] % 2]
                        pj_cnt[0] += 1
                        proj_part(wslot, wkey, tg, bank, 0, 8)
                        yield
                        proj_part(wslot, wkey, tg, bank, 8, 16)
                        yield
                        qkv_evac(h, which, tg, bank)
                        if which == 2:
                            yield
                            v_transpose(h, tg)
                            yield
                            v_evac(h, tg)
                        yield

            steps = []
            for h in range(8):
                for tg in range(NTG):
                    nb = 4 * tg + 4
                    for j, sbk in enumerate(range(nb - 1, -1, -1)):
                        i_d = sbk - 4 * tg
                        c0 = max(i_d, 0) * 128
                        steps.append(dict(h=h, tg=tg, sb=sbk, c0=c0, diag=(i_d >= 0), first=(j == 0), last=(sbk == 0)))

            def st_A1(n, s):
                h, tg, sbk, c0 = s["h"], s["tg"], s["sb"], s["c0"]
                hp = h % 2
                t0 = tg * TG
                zb = B_Z[n % 2]
                eb, spb = e_b[n % NE], sp_b[n % NSP]
                P.op("pe", lambda e: e.matmul(ps[:, zb, c0:TG], lhsT=kT[hp][:, sbk * 128:(sbk + 1) * 128],
                                              rhs=qT[hp][:, t0 + c0:t0 + TG], start=True, stop=True),
                     reads=[("kT", hp, sbk // 4), ("qT", hp, tg)], writes=[("ps", zb)], cost=0.3)
                P.op("act", lambda e: e.activation(out=eb[:, c0:TG], in_=ps[:, zb, c0:TG], func=AF.Exp, scale=-SCALE),
                     writes=[("ps", zb), ("e", n % NE)])
                P.op("act", lambda e: e.activation(out=spb[:, c0:TG], in_=eb[:, c0:TG], func=AF.Ln, bias=1.0),
                     reads=[("e", n % NE)], writes=[("sp", n % NSP)])

            def st_A2(n, s):
                c0 = s["c0"]
                zb = B_Z[n % 2]
                spb, pmb = sp_b[n % NSP], pm_b[n % NPM]
                P.op("dve", lambda e: e.scalar_tensor_tensor(out=pmb[:, c0:TG], in0=ps[:, zb, c0:TG], scalar=SCALE,
                                                             in1=spb[:, c0:TG], op0=ALU.mult, op1=ALU.add),
                     reads=[("sp", n % NSP)], writes=[("ps", zb), ("pm", n % NPM)])
                if s["diag"]:
                    P.op("dve", lambda e: e.tensor_tensor(out=pmb[:, c0:c0 + 128], in0=pmb[:, c0:c0 + 128], in1=maskd, op=ALU.mult),
                         reads=[("cm",), ("pm", n % NPM)], writes=[("pm", n % NPM)])

            def st_B(n, s):
                c0 = s["c0"]
                spb, pmb, ttb, atb = sp_b[n % NSP], pm_b[n % NPM], tt_b[n % NTT], at_b[n % NAT]
                first = s["first"]
                pc0 = steps[n - 1]["c0"] if not first else 0
                ppm = pm_b[(n - 1) % NPM]

                def mm(e):
                    if not first:
                        e.matmul(ps[:, B_ACC, pc0:TG], lhsT=ltri, rhs=ppm[:, pc0:TG], start=False, stop=False,
                                 skip_group_check=True)
                    return e.matmul(ps[:, B_ACC, c0:TG], lhsT=tri, rhs=pmb[:, c0:TG], start=first, stop=s["last"],
                                    skip_group_check=True)
                rd = [("pm", n % NPM), ("cm",)]
                if not first:
                    rd.append(("pm", (n - 1) % NPM))
                P.op("pe", mm, reads=rd, writes=[("ps", B_ACC)], cost=0.55)
                P.op("dve", lambda e: e.tensor_tensor(out=ttb[:, c0:TG], in0=ps[:, B_ACC, c0:TG], in1=spb[:, c0:TG], op=ALU.add),
                     reads=[("sp", n % NSP)], writes=[("ps", B_ACC), ("tt", n % NTT)])

            def st_B2(n, s):
                c0 = s["c0"]
                ttb, atb = tt_b[n % NTT], at_b[n % NAT]
                P.op("act", lambda e: e.activation(out=atb[:, c0:TG], in_=ttb[:, c0:TG], func=AF.Exp, scale=-1.0),
                     reads=[("tt", n % NTT)], writes=[("at", n % NAT)])

            def st_B3(n, s):
                c0 = s["c0"]
                atb = at_b[n % NAT]
                if s["diag"]:
                    P.op("dve", lambda e: e.tensor_tensor(out=atb[:, c0:c0 + 128], in0=atb[:, c0:c0 + 128], in1=maskd, op=ALU.mult),
                         reads=[("cm",), ("at", n % NAT)], writes=[("at", n % NAT)])

            ch_cnt = [0]
            deferred = []

            def chain_end1(par):
                ob, sqb = osb[par], asq[par]
                P.op("dve", lambda e: e.tensor_copy(out=ob[:, :], in_=bankv(B_AV)), writes=[("ps", B_AV), ("osb", par)])
                P.op("act", lambda e: e.activation(out=sqb[:, :], in_=ob[:, :], func=AF.Square),
                     reads=[("osb", par)], writes=[("asq", par)])

            def chain_end5(h, tg, par):
                t0 = tg * TG
                ob = osb[par]
                P.op("dve", lambda e: e.scalar_tensor_tensor(
                    out=oT[:, 8 + h, t0:t0 + TG], in0=ob[:, :], scalar=vecs[:, V_GON + 8 + h:V_GON + 9 + h],
                    in1=bankv(B_BC), op0=ALU.mult, op1=ALU.mult),
                    reads=[("osb", par), ("vecs",)], writes=[("ps", B_BC), ("oT", 8 + h, tg)])

            def st_D(n, s, it):
                h, tg, sbk, c0 = s["h"], s["tg"], s["sb"], s["c0"]
                hp = h % 2
                atb = at_b[n % NAT]
                P.op("pe", lambda e: e.matmul(ps[:, B_AV, c0:TG], lhsT=vtok[hp][:, sbk, :], rhs=atb[:, c0:TG],
                                              start=s["first"], stop=s["last"], skip_group_check=True),
                     reads=[("at", n % NAT), ("vtok", hp, sbk // 4)], writes=[("ps", B_AV)], cost=0.28)
                if s["last"]:
                    par = ch_cnt[0] % 3
                    ch_cnt[0] += 1
                    box = {}

                    def f1(par=par):
                        chain_end1(par)

                    def f2(par=par, box=box):
                        box["slot"] = rstd_p1([asq[par]], [("asq", par)], 128.0)

                    def f3(box=box):
                        rstd_p1b(box["slot"])

                    def f4(box=box):
                        rstd_p2(box["slot"])

                    def f5(h=h, tg=tg, par=par):
                        chain_end5(h, tg, par)
                    f1()
                    for dly, f in ((1, f2), (2, f3), (3, f4), (4, f5)):
                        deferred.append((it + dly, f))

            def attn_thread():
                N = len(steps)
                it = 0
                while True:
                    busy = False
                    if 0 <= it - 3 < N:
                        st_B(it - 3, steps[it - 3]); busy = True
                    if it < N:
                        st_A1(it, steps[it]); busy = True
                    if 0 <= it - 1 < N:
                        st_A2(it - 1, steps[it - 1]); busy = True
                    if 0 <= it - 3 < N:
                        st_B2(it - 3, steps[it - 3])
                    if 0 <= it - 4 < N:
                        st_B3(it - 4, steps[it - 4])
                        st_D(it - 4, steps[it - 4], it); busy = True
                    for d in [d for d in deferred if d[0] <= it]:
                        d[1]()
                        deferred.remove(d)
                        busy = True
                    if deferred:
                        busy = True
                    if not busy:
                        break
                    it += 1
                    yield

            for _ in qkv_thread(0):
                pass
            ag = attn_thread()
            per_head = len(steps) // 8
            for h in range(8):
                qg = qkv_thread(h + 1) if h + 1 < 8 else None
                for i in range(per_head):
                    next(ag)
                    if qg is not None:
                        next(qg, None)
                if qg is not None:
                    for _ in qg:
                        pass
            for _ in ag:
                pass

        if upto >= 4:
            P.barrier()
            AR.release(0)
            x1 = AR.alloc(NCH * TG * 4, F32, (NCH, TG))
            h2T = AR.alloc(NCH * TG * 2, BF16, (NCH, TG))
            actT = AR.alloc(NFF * TG * 2, BF16, (NFF, TG))
            NW3 = 3
            wslots3 = [AR.alloc(11 * 512 * 2, BF16) for _ in range(NW3)]
            s_sb = [AR.alloc(TG * 2, BF16) for _ in range(2)]
            tmp3 = [AR.alloc(TG * 4, F32) for _ in range(2)]
            ostg = [AR.alloc(TG * 4, F32) for _ in range(2)]
            loads = []
            for st in range(NTG):
                loads += [w_out_r[i] for i in range(8)]
                for cg in range(22):
                    loads += [w_gate_r[cg], w_up_r[cg]]
                loads += [w_down_r[i] for i in range(16)]
            ws3 = WStream("w3", wslots3, loads, depth=1)
            wi = [0]
            rr = [0]

            def nb3():
                b = rr[0]
                rr[0] = (b + 1) % 6
                return b

            def mm_piece(banks, wslot, wkey, ncb, kc, rhs_of, rhs_keys, k0, ktot, split=False):
                rhs_list = [rhs_of(k0 + k) for k in range(kc)]
                wc = ncb * 128

                def mm(e):
                    for cb_ in range(ncb):
                        for k in range(kc):
                            kk = k0 + k
                            ins = e.matmul(bankv(banks[cb_]), lhsT=wslot[:, k * wc + cb_ * 128:k * wc + (cb_ + 1) * 128],
                                           rhs=rhs_list[k], start=(kk == 0), stop=(kk == ktot - 1), skip_group_check=True)
                    return ins
                if not split:
                    def mcb(cb_):
                        def mmc(e):
                            for k in range(kc):
                                kk = k0 + k
                                ins = e.matmul(bankv(banks[cb_]), lhsT=wslot[:, k * wc + cb_ * 128:k * wc + (cb_ + 1) * 128],
                                               rhs=rhs_list[k], start=(kk == 0), stop=(kk == ktot - 1), skip_group_check=True)
                            return ins
                        P.op("pe", mmc, reads=[wkey] + list(rhs_keys), writes=[("ps", banks[cb_])], cost=kc * 0.26)
                    for cb_ in range(ncb):
                        mcb(cb_)
                    return

                def mk(k):
                    kk = k0 + k

                    def mmk(e):
                        for cb_ in range(ncb):
                            ins = e.matmul(bankv(banks[cb_]), lhsT=wslot[:, k * wc + cb_ * 128:k * wc + (cb_ + 1) * 128],
                                           rhs=rhs_list[k], start=(kk == 0), stop=(kk == ktot - 1), skip_group_check=True)
                        return ins
                    P.op("pe", mmk, reads=[wkey, rhs_keys[k]], writes=[("ps", b) for b in banks], cost=ncb * 0.26)
                for k in range(kc):
                    mk(k)

            def ld_x(q, t0):
                for k in range(4 * q, 4 * q + 4):
                    ld_x1(k, t0)

            def ld_x1(k, t0):
                P.op("sp", lambda e: e.dma_start(out=x1[:, k, :], in_=xT_v[:, k, t0:t0 + TG]),
                     writes=[("x1", k)], dma=True, cost=3.0)

            def resid(bank, db, gcol):
                P.op("dve", lambda e: e.scalar_tensor_tensor(out=x1[:, db, :], in0=bankv(bank), scalar=gcol,
                                                             in1=x1[:, db, :], op0=ALU.mult, op1=ALU.add),
                     reads=[("modT", 2), ("modT", 5)], writes=[("ps", bank), ("x1", db)])

            def sq_x(db):
                P.op("act", lambda e: e.activation(out=actT[:, db, :], in_=x1[:, db, :], func=AF.Square),
                     reads=[("x1", db)], writes=[("actT", db)])

            def norm2(db):
                tb = tmp3[db % 2]
                P.op("dve", lambda e: e.tensor_tensor(out=tb[:, :], in0=x1[:, db, :], in1=bankv(B_BC), op=ALU.mult),
                     reads=[("x1", db)], writes=[("tmp3", db % 2), ("ps", B_BC)])
                P.op("act", lambda e: e.activation(out=h2T[:, db, :], in_=tb[:, :], func=AF.Identity,
                                                   scale=G2[:, db:db + 1], bias=SH2[:, db:db + 1]),
                     reads=[("tmp3", db % 2), ("G2",), ("modT", 3)], writes=[("h2T", db)])

            def swiglu(gb, ub, fb):
                sb_ = s_sb[fb % 2]
                P.op("act", lambda e: e.activation(out=sb_[:, :], in_=bankv(gb), func=AF.Silu),
                     writes=[("ps", gb), ("s_sb", fb % 2)])
                P.op("dve", lambda e: e.tensor_tensor(out=actT[:, fb, :], in0=bankv(ub), in1=sb_[:, :], op=ALU.mult),
                     reads=[("s_sb", fb % 2)], writes=[("ps", ub), ("actT", fb)])

            def final(db, t0):
                j = db % 4
                ob = tmp3[j] if j < 2 else ostg[j - 2]
                key = ("tmp3", j) if j < 2 else ("ostg", j - 2)
                P.op("dve", lambda e: e.scalar_tensor_tensor(out=ob[:, :], in0=x1[:, db, :],
                                                             scalar=vecs[:, V_GF + db:V_GF + db + 1], in1=bankv(B_BC),
                                                             op0=ALU.mult, op1=ALU.mult),
                     reads=[("vecs",), ("x1", db)], writes=[key, ("ps", B_BC)])
                out_dmas.append(P.op("sp", lambda e: e.dma_start(out=outT_v[:, db, t0:t0 + TG], in_=ob[:, :]),
                                     reads=[key], dma=True, cost=3.0))

            def st_out(q, t0):
                pass

            def phase3():
                for st in range(NTG):
                    t0 = st * TG
                    for q in range(4):
                        ld_x(q, t0)
                    for ng in range(4):
                        banks = [nb3() for _ in range(4)]
                        for kp in range(2):
                            wslot, wkey = ws3.get(wi[0]); wi[0] += 1
                            mm_piece(banks, wslot, wkey, 4, 8, lambda kk: oT[:, kk, t0:t0 + TG],
                                     [("oT", kk, st) for kk in range(8 * kp, 8 * kp + 8)], 8 * kp, 16)
                        for cb_ in range(4):
                            db = ng * 4 + cb_
                            resid(banks[cb_], db, GT1[:, db:db + 1])
                            sq_x(db)
                        yield
                    rstd_bcast([actT[:, db, :] for db in range(NCH)], [("actT", db) for db in range(NCH)], float(D))
                    for db in range(NCH):
                        norm2(db)
                    yield
                    for cg in range(22):
                        gbanks = [nb3() for _ in range(2)]
                        wslot, wkey = ws3.get(wi[0]); wi[0] += 1
                        mm_piece(gbanks, wslot, wkey, 2, 16, lambda kk: h2T[:, kk, :], [("h2T", kk) for kk in range(16)], 0, 16,
                                 split=(cg == 0))
                        ubanks = [nb3() for _ in range(2)]
                        wslot, wkey = ws3.get(wi[0]); wi[0] += 1
                        mm_piece(ubanks, wslot, wkey, 2, 16, lambda kk: h2T[:, kk, :], [("h2T", kk) for kk in range(16)], 0, 16,
                                 split=(cg == 0))
                        for j in range(2):
                            swiglu(gbanks[j], ubanks[j], cg * 2 + j)
                        yield
                    for ng in range(4):
                        banks = [nb3() for _ in range(4)]
                        for kp in range(4):
                            wslot, wkey = ws3.get(wi[0]); wi[0] += 1
                            mm_piece(banks, wslot, wkey, 4, 11, lambda kk: actT[:, kk, :],
                                     [("actT", kk) for kk in range(11 * kp, 11 * kp + 11)], 11 * kp, 44)
                        for cb_ in range(4):
                            db = ng * 4 + cb_
                            resid(banks[cb_], db, GT2[:, db:db + 1])
                        yield
                    for db in range(NCH):
                        sq_x(db)
                    rstd_bcast([actT[:, db, :] for db in range(NCH)], [("actT", db) for db in range(NCH)], float(D))
                    for db in range(NCH):
                        final(db, t0)
                    for q in range(4):
                        st_out(q, t0)
                    yield

            for _ in phase3():
                pass

        if debug:
            P.barrier()
            if upto < 4:
                out_dmas.append(P.op("pool", lambda e: e.dma_start(out=dbg["hT"], in_=hT.rearrange("p a b -> p (a b)")),
                                     reads=[], dma=True))
            out_dmas.append(P.op("pool", lambda e: e.dma_start(out=dbg["oT"], in_=oT[:, :, :].rearrange("p a b -> p (a b)")),
                                 reads=[], dma=True))
            out_dmas.append(P.op("sp", lambda e: e.dma_start(out=dbg["mod"], in_=modT[:, :]), reads=[], dma=True))

        est = P.schedule(reorder=REORDER)
        print("arena high-water", AR.hi, "ops", {e: len(v) for e, v in P.streams.items()}, "sched est us", round(est))
        with nc.Block() as block:
            @block.tensor
            def _(e):
                P.replay("pe", e)

            @block.scalar
            def _(e):
                P.replay("act", e)

            @block.vector
            def _(e):
                P.replay("dve", e)

            @block.gpsimd
            def _(e):
                w = P.replay("pool", e)
                P.final_wait("pool", e, w, [o for o in out_dmas if o.eng == "pool"])

            @block.sync
            def _(e):
                w = P.replay("sp", e)
                P.final_wait("sp", e, w, out_dmas)
    return nc


def _col_layout(v, n):
    return np.ascontiguousarray(np.asarray(v, np.float32).reshape(n, 128).T)


def _tile_w(w, ncols, kc):
    K, N = w.shape
    nk = K // 128
    a = w.reshape(nk // kc, kc, 128, N // ncols, ncols)
    a = a.transpose(3, 0, 2, 1, 4)
    return np.ascontiguousarray(a.reshape((N // ncols) * (nk // kc), 128, kc * ncols))


def prepare_inputs(x, c, w_ada, b_ada, norm1_g, w_in, v_norm_g, w_spatial, b_spatial,
                   out_norm_g, w_out, norm2_g, w_gate, w_up, w_down, final_g):
    f = lambda a: np.asarray(a, np.float32)
    x, c = f(x), f(c)
    wadaT = np.ascontiguousarray(f(w_ada)[0].T)
    vecs = np.zeros((128, NVEC), np.float32)
    vecs[:, V_BADA:V_BADA + 96] = _col_layout(f(b_ada)[0], 96)
    vecs[:, V_G1:V_G1 + 16] = _col_layout(f(norm1_g)[0], 16)
    vecs[:, V_G2:V_G2 + 16] = _col_layout(f(norm2_g)[0], 16)
    vecs[:, V_GF:V_GF + 16] = _col_layout(f(final_g), 16)
    vecs[:, V_GON:V_GON + 16] = _col_layout(f(out_norm_g)[0], 16)
    vecs[:, V_GV:V_GV + 8] = _col_layout(f(v_norm_g)[0], 8)
    bsp = np.ascontiguousarray(f(b_spatial)[0].reshape(1, 1024))
    wsT = np.ascontiguousarray(f(w_spatial)[0].transpose(2, 0, 1).reshape(128, 1024))
    p = np.arange(128)[:, None]
    q = np.arange(128)[None, :]
    cmask = np.concatenate([(p > q), (p <= q), (q > p), (p == q)], axis=1).astype(np.float32)
    w_in0 = f(w_in)[0]
    cols = []
    for g in range(8):
        cols += [g * 128, 1024 + g * 128]
    for h in range(8):
        cols += [2048 + h * 128, 3072 + h * 128, 4096 + h * 128]
    w_in_r = np.empty((40, 128, 2048), np.float32)
    for i, c0 in enumerate(cols):
        blk = w_in0[:, c0:c0 + 128].reshape(16, 128, 128)
        w_in_r[i] = blk.transpose(1, 0, 2).reshape(128, 2048)
    shared = dict(
        wadaT=wadaT, vecs=vecs, bsp=bsp, wsT=wsT, cmask=cmask, w_in_r=w_in_r,
        w_out_r=_tile_w(f(w_out)[0], 512, 8),
        w_gate_r=_tile_w(f(w_gate)[0], 256, 16),
        w_up_r=_tile_w(f(w_up)[0], 256, 16),
        w_down_r=_tile_w(f(w_down)[0], 512, 11),
    )
    in_maps = []
    for b in range(x.shape[0]):
        m = dict(shared)
        m["xT"] = np.ascontiguousarray(x[b].T)
        m["cb"] = np.ascontiguousarray(c[b:b + 1])
        in_maps.append(m)
    return in_maps


def kernel(**inputs):
    in_maps = prepare_inputs(**inputs)
    nc = build_nc()
    res = run_bass_kernel_spmd(nc, in_maps, core_ids=list(range(len(in_maps))))
    out = np.stack([np.ascontiguousarray(r["outT"].T) for r in res.results], axis=0)
    return out.astype(np.float32)
```

```python
from contextlib import ExitStack

import numpy as np

import concourse.bass as bass
import concourse.mybir as mybir
from concourse.bass_utils import run_bass_kernel_spmd

F32 = mybir.dt.float32
BF16 = mybir.dt.bfloat16
AF = mybir.ActivationFunctionType
ALU = mybir.AluOpType

D = 2048
S = 2048
NCH = 16
DFF = 5632
NFF = 44
NMOD = 96
EPS = 1e-6
SCALE = 128.0 ** -0.5
TG = 512
NTG = S // TG

V_BADA, V_G1, V_G2, V_GF, V_GON, V_GV = 0, 96, 112, 128, 144, 160
NVEC = 168

ARENA_BYTES = 136 * 1024
REORDER = True


class Op:
    __slots__ = ("eng", "fn", "reads", "writes", "dma", "cost", "idx", "deps", "succs", "needs_signal", "sig",
                 "nwait", "fin")

    def __init__(self, eng, fn, reads, writes, dma, cost, idx):
        self.eng = eng
        self.fn = fn
        self.reads = tuple(reads)
        self.writes = tuple(writes)
        self.dma = dma
        self.cost = cost
        self.idx = idx
        self.deps = []
        self.succs = []
        self.needs_signal = dma
        self.sig = None
        self.nwait = 0
        self.fin = 0.0


DEFAULT_COST = {"pe": 0.3, "act": 0.65, "dve": 0.65, "pool": 1.0, "sp": 3.0}
SCHED_LAT = 0.2
SCHED_Q = 0.3
SCHED_LEVEL = True


class Prog:
    ENGS = ("pe", "act", "dve", "pool", "sp")

    def __init__(self, nc, es, n_dma_sems=40):
        self.nc = nc
        self.segs = [[]]
        self.nops = 0
        self.streams = {e: [] for e in self.ENGS}
        self.sems = {e: es.enter_context(nc.semaphore("s_" + e)) for e in self.ENGS}
        self.dma_sems = [es.enter_context(nc.semaphore("d%d" % i)) for i in range(n_dma_sems)]

    def op(self, eng, fn, reads=(), writes=(), dma=False, cost=None, name=""):
        if cost is None:
            cost = DEFAULT_COST["sp" if dma else eng]
        o = Op(eng, fn, reads, writes, dma, cost, self.nops)
        self.nops += 1
        self.segs[-1].append(o)
        return o

    def barrier(self):
        if self.segs[-1]:
            self.segs.append([])

    @staticmethod
    def _build_deps(seg):
        lastw = {}
        readers = {}
        for o in seg:
            deps = {}
            for k in o.reads:
                w = lastw.get(k)
                if w is not None:
                    deps[id(w)] = w
            for k in o.writes:
                w = lastw.get(k)
                if w is not None:
                    deps[id(w)] = w
                for r in readers.get(k, ()):
                    deps[id(r)] = r
            deps.pop(id(o), None)
            o.deps = list(deps.values())
            for d in o.deps:
                d.succs.append(o)
            for k in o.reads:
                readers.setdefault(k, []).append(o)
            for k in o.writes:
                lastw[k] = o
                readers[k] = []

    def _list_schedule(self, seg):
        level = {}
        for o in reversed(seg):
            m = 0.0
            for s_ in o.succs:
                lv = level[id(s_)]
                if lv > m:
                    m = lv
            level[id(o)] = m + o.cost + SCHED_LAT
        indeg = {id(o): len(o.deps) for o in seg}
        ready = {e: [] for e in self.ENGS}
        for o in seg:
            if not o.deps:
                ready[o.eng].append(o)
        t_free = {e: 0.0 for e in self.ENGS}
        order = []
        n = len(seg)
        while len(order) < n:
            best = None
            for e in self.ENGS:
                lst = ready[e]
                if not lst:
                    continue
                tf = t_free[e]
                for o in lst:
                    st = tf
                    for d in o.deps:
                        f = d.fin + SCHED_LAT
                        if f > st:
                            st = f
                    key = (int(st / SCHED_Q), -level[id(o)] if SCHED_LEVEL else 0.0, o.idx)
                    if best is None or key < best[0]:
                        best = (key, o, st)
            _, o, st = best
            e = o.eng
            ready[e].remove(o)
            o.fin = st + o.cost
            t_free[e] = st + (0.5 if o.dma else o.cost)
            order.append(o)
            for s_ in o.succs:
                indeg[id(s_)] -= 1
                if indeg[id(s_)] == 0:
                    ready[s_.eng].append(s_)
        return order, max(t_free.values())

    def schedule(self, reorder=True):
        prev_lasts = []
        last_compute = {}
        nsem = len(self.dma_sems)
        dma_last = [None] * nsem
        dma_cnt = [0] * nsem
        half = nsem // 2
        rr_q = {"sp": 0, "pool": 0}
        base_q = {"sp": 0, "pool": half}
        est_total = 0.0
        for seg in self.segs:
            self._build_deps(seg)
            if reorder:
                order, est = self._list_schedule(seg)
                est_total += est
            else:
                order = seg
            seen = set()
            seg_dmas = []
            for o in order:
                if prev_lasts and o.eng not in seen:
                    o.deps = o.deps + [d for d in prev_lasts if d is not o]
                seen.add(o.eng)
                if o.dma:
                    rr = base_q[o.eng] + rr_q[o.eng]
                    rr_q[o.eng] = (rr_q[o.eng] + 1) % half
                    prev = dma_last[rr]
                    if prev is not None:
                        o.deps.append(prev)
                    dma_cnt[rr] += 1
                    o.sig = (self.dma_sems[rr], 16 * dma_cnt[rr])
                    dma_last[rr] = o
                    seg_dmas.append(o)
                else:
                    last_compute[o.eng] = o
                self.streams[o.eng].append(o)
            prev_lasts = list(last_compute.values()) + seg_dmas
        for e in self.ENGS:
            for o in self.streams[e]:
                kept = []
                for d in o.deps:
                    if d.dma or o.dma or d.eng != o.eng or o.eng != "pe":
                        kept.append(d)
                        d.needs_signal = True
                o.deps = kept
        for e in self.ENGS:
            cnt = 0
            for o in self.streams[e]:
                if o.dma:
                    continue
                if o.needs_signal:
                    cnt += 1
                    o.sig = (self.sems[e], cnt)
        return est_total

    def replay(self, eng_name, eng):
        waited = {}
        for o in self.streams[eng_name]:
            for d in o.deps:
                sem, val = d.sig
                key = id(sem)
                if waited.get(key, 0) < val:
                    eng.wait_ge(sem, val)
                    waited[key] = val
            ins = o.fn(eng)
            if o.dma:
                ins.then_inc(o.sig[0], 16)
            elif o.needs_signal:
                ins.then_inc(o.sig[0], 1)
        return waited

    def final_wait(self, eng_name, eng, waited, ops):
        for d in ops:
            sem, val = d.sig
            if waited.get(id(sem), 0) < val:
                eng.wait_ge(sem, val)
                waited[id(sem)] = val


class Arena:
    def __init__(self, t, nbytes):
        self.t = t
        self.nbytes = nbytes
        self.top = 0
        self.hi = 0

    def mark(self):
        return self.top

    def release(self, m):
        self.top = m

    def alloc(self, nbytes, dtype=F32, shape=None):
        assert nbytes % 4 == 0
        off = self.top
        self.top += (nbytes + 63) // 64 * 64
        self.hi = max(self.hi, self.top)
        assert self.top <= self.nbytes, "arena overflow %d > %d" % (self.top, self.nbytes)
        v = self.t[:, off // 4:(off + nbytes) // 4]
        if dtype == BF16:
            v = v.bitcast(BF16)
        if shape is not None:
            names = " ".join("a%d" % i for i in range(len(shape)))
            kw = {"a%d" % i: int(s) for i, s in enumerate(shape)}
            v = v.rearrange("p (%s) -> p %s" % (names, names), **kw)
        return v


def roundrobin(gens, weights):
    gens = list(gens)
    alive = [True] * len(gens)
    while any(alive):
        for i, g in enumerate(gens):
            if not alive[i]:
                continue
            for _ in range(weights[i]):
                try:
                    next(g)
                except StopIteration:
                    alive[i] = False
                    break


def build_nc(upto=99, debug=False):
    nc = bass.Bass("TRN2", target_bir_lowering=False)
    dt_in = lambda name, shape: nc.dram_tensor(name, list(shape), F32, kind="ExternalInput").ap()
    xT = dt_in("xT", [D, S])
    cb = dt_in("cb", [1, D])
    wadaT = dt_in("wadaT", [NMOD * 128, D])
    vecs_d = dt_in("vecs", [128, NVEC])
    bsp_d = dt_in("bsp", [1, 1024])
    wsT_d = dt_in("wsT", [128, 1024])
    cmask_d = dt_in("cmask", [128, 512])
    w_in_r = dt_in("w_in_r", [40, 128, 2048])
    w_out_r = dt_in("w_out_r", [8, 128, 8 * 512])
    w_gate_r = dt_in("w_gate_r", [22, 128, 8 * 512])
    w_up_r = dt_in("w_up_r", [22, 128, 8 * 512])
    w_down_r = dt_in("w_down_r", [16, 128, 11 * 512])
    outT = nc.dram_tensor("outT", [D, S], F32, kind="ExternalOutput").ap()
    dbg = {}
    if debug:
        dbg["hT"] = nc.dram_tensor("dbg_hT", [128, NCH * S], F32, kind="ExternalOutput").ap()
        dbg["oT"] = nc.dram_tensor("dbg_oT", [128, NCH * S], F32, kind="ExternalOutput").ap()
        dbg["mod"] = nc.dram_tensor("dbg_mod", [128, NMOD], F32, kind="ExternalOutput").ap()

    xT_v = xT.rearrange("(k p) s -> p k s", p=128)
    outT_v = outT.rearrange("(k p) s -> p k s", p=128)

    with ExitStack() as es:
        P = Prog(nc, es)
        sb = lambda name, shape, dt: es.enter_context(nc.sbuf_tensor("sb_" + name, list(shape), dt))
        ps = es.enter_context(nc.psum_tensor("ps", [128, 8, 512], F32))
        cm = sb("cm", [128, 512], BF16)
        cmf = sb("cmf", [128, 128], F32)
        ones = sb("ones", [128, 128], BF16)
        onesf = sb("onesf", [128, 128], F32)
        vecs = sb("vecs", [128, NVEC], F32)
        modT = sb("modT", [128, NMOD], F32)
        gmod = sb("gmod", [128, 32], F32)
        mhalf = sb("mhalf", [128, 16], F32)
        st_r = sb("st_r", [128, 2, 4], F32)
        biasT = sb("biasT", [128, 40], F32)
        sh1b = sb("sh1b", [128, 16], BF16)
        st_rb = sb("st_rb", [128, 2, 512], F32)
        oT = sb("oT", [128, NCH, S], BF16)
        arena_t = sb("arena", [128, ARENA_BYTES // 4], F32)
        AR = Arena(arena_t, ARENA_BYTES)

        tri = cm[:, 0:128]
        ltri = cm[:, 128:256]
        maskd = cm[:, 256:384]
        ident = cm[:, 384:512]
        G1 = gmod[:, 0:16]
        G2 = gmod[:, 16:32]
        SH1, SC1, GT1, SH2, SC2, GT2 = (modT[:, 16 * i:16 * (i + 1)] for i in range(6))

        def bankv(b):
            return ps[:, b, :]

        B_ST, B_BC = 6, 7
        rs_cnt = [0]

        def rstd_p1(srcs, keys, n, nt=4):
            slot = rs_cnt[0] % 2
            rs_cnt[0] += 1
            r = st_r[:, slot, 0:nt]
            nsrc = len(srcs)

            def mm(e):
                for i in range(nt):
                    for c, s_ap in enumerate(srcs):
                        ins = e.matmul(ps[:, B_ST, i:i + 1], lhsT=s_ap[:, i * 128:(i + 1) * 128], rhs=ones[:, 0:1],
                                       start=(c == 0), stop=(c == nsrc - 1), skip_group_check=True)
                return ins
            P.op("pe", mm, reads=[("ones",)] + list(keys), writes=[("ps", B_ST)], cost=0.1 + 0.07 * nt * nsrc)
            P.op("dve", lambda e: e.tensor_scalar(out=r, in0=ps[:, B_ST, 0:nt], scalar1=1.0 / n, scalar2=EPS,
                                                  op0=ALU.mult, op1=ALU.add),
                 writes=[("ps", B_ST), ("st_r", slot)], cost=0.2)
            P.op("pool", lambda e: e.tensor_tensor(out=r, in0=r, in1=mhalf[:, 0:nt], op=ALU.pow),
                 reads=[("mhalf",)], writes=[("st_r", slot)], cost=1.1)
            return slot

        def rstd_p1b(slot, nt=4):
            r = st_r[:, slot, :]
            rb = st_rb[:, slot, :]

            def bc(e):
                for i in range(nt):
                    ins = e.activation(out=rb[:, i * 128:(i + 1) * 128], in_=onesf[:, :], func=AF.Copy, scale=r[:, i:i + 1])
                return ins
            P.op("act", bc, reads=[("st_r", slot), ("onesf",)], writes=[("st_rb", slot)], cost=0.38 * nt)

        def rstd_p2(slot, nt=4):
            rb = st_rb[:, slot, :]

            def mm2(e):
                for i in range(nt):
                    ins = e.matmul(ps[:, B_BC, i * 128:(i + 1) * 128], lhsT=rb[:, i * 128:(i + 1) * 128], rhs=cmf[:, :],
                                   start=True, stop=True, skip_group_check=True)
                return ins
            P.op("pe", mm2, reads=[("st_rb", slot), ("cm",)], writes=[("ps", B_BC)], cost=0.35 * nt)

        def rstd_bcast(srcs, keys, n, nt=4):
            slot = rstd_p1(srcs, keys, n, nt)
            rstd_p1b(slot, nt)
            rstd_p2(slot, nt)

        hT = AR.alloc(NCH * S * 2, BF16, (NCH, S))
        m_r1 = AR.mark()
        cact = AR.alloc(D * 4, F32)
        junk = AR.alloc(D * 2, BF16)
        NWA = 3
        wa = [AR.alloc(D * 4, F32) for _ in range(NWA)]
        m_r1b = AR.mark()

        P.op("pool", lambda e: e.dma_start(out=cm[:, :], in_=cmask_d), writes=[("cm",)], dma=True)
        P.op("sp", lambda e: e.dma_start(out=cmf[:, :], in_=cmask_d[:, 384:512]), writes=[("cm",)], dma=True)
        P.op("sp", lambda e: e.dma_start(out=vecs[:, :], in_=vecs_d), writes=[("vecs",)], dma=True)
        P.op("sp", lambda e: e.dma_start(out=cact[:, :], in_=cb.partition_broadcast(128)),
             writes=[("cact",)], dma=True)
        P.op("pool", lambda e: e.memset(ones[:, :], 1.0), writes=[("ones",)])
        P.op("pool", lambda e: e.memset(onesf[:, :], 1.0), writes=[("onesf",)])
        P.op("pool", lambda e: e.memset(mhalf[:, :], -0.5), writes=[("mhalf",)])
        P.op("act", lambda e: e.activation(out=cact[:, :], in_=cact[:, :], func=AF.Silu),
             reads=[("cact",)], writes=[("cact",)])

        def mod_block(j):
            slot = j % NWA
            P.op("sp", lambda e: e.dma_start(out=wa[slot][:, :], in_=wadaT[j * 128:(j + 1) * 128, :]),
                 writes=[("wa", slot)], dma=True)
            P.op("dve", lambda e: e.scalar_tensor_tensor(
                out=junk[:, :], in0=wa[slot][:, :], scalar=1.0, in1=cact[:, :],
                op0=ALU.mult, op1=ALU.mult, accum_out=modT[:, j:j + 1]),
                reads=[("wa", slot), ("cact",)], writes=[("junk",), ("modT", j // 16)], cost=2.35)

        def mod_finish(lo, hi):
            P.op("dve", lambda e: e.tensor_tensor(
                out=modT[:, lo * 16:hi * 16], in0=modT[:, lo * 16:hi * 16],
                in1=vecs[:, V_BADA + lo * 16:V_BADA + hi * 16], op=ALU.add),
                reads=[("modT", i) for i in range(lo, hi)] + [("vecs",)],
                writes=[("modT", i) for i in range(lo, hi)])

        def gmod_make(dst, sc_idx, gcol, key):
            P.op("dve", lambda e: e.scalar_tensor_tensor(
                out=dst, in0=modT[:, sc_idx * 16:(sc_idx + 1) * 16], scalar=1.0,
                in1=vecs[:, gcol:gcol + 16], op0=ALU.add, op1=ALU.mult),
                reads=[("modT", sc_idx), ("vecs",)], writes=[key])

        for j in range(32):
            mod_block(j)
        mod_finish(0, 2)
        gmod_make(G1, 1, V_G1, ("G1",))
        P.op("dve", lambda e: e.tensor_copy(out=sh1b[:, :], in_=SH1), reads=[("modT", 0)], writes=[("sh1b",)], cost=0.2)

        def mod2_thread():
            for j in range(32, NMOD):
                mod_block(j)
                yield
            mod_finish(2, 6)
            gmod_make(G2, 4, V_G2, ("G2",))
            yield

        HG = 256
        xt2 = [AR.alloc(NCH * HG * 4, F32, (NCH, HG)) for _ in range(2)]
        rsd = [AR.alloc(HG * 4, F32) for _ in range(2)]

        def p1a_load(q, hg, t0):
            xt = xt2[hg % 2]
            P.op("sp", lambda e: e.dma_start(out=xt[:, 4 * q:4 * q + 4, :],
                                             in_=xT_v[:, 4 * q:4 * q + 4, t0:t0 + HG]),
                 writes=[("xt", hg % 2, k) for k in range(4 * q, 4 * q + 4)], dma=True, cost=4.0)

        def p1a_square(k, hg, t0):
            xt = xt2[hg % 2]
            P.op("act", lambda e: e.activation(out=oT[:, k, t0:t0 + HG], in_=xt[:, k, :], func=AF.Square),
                 reads=[("xt", hg % 2, k)], writes=[("oT", k, hg)], cost=0.45)

        def p1a_rstd(hg, t0):
            rb_ = rsd[hg % 2]

            def mm(e):
                for k in range(NCH):
                    ins = e.matmul(ps[:, B_BC, 0:HG], lhsT=ones[:, :], rhs=oT[:, k, t0:t0 + HG],
                                   start=(k == 0), stop=(k == NCH - 1))
                return ins
            P.op("pe", mm, reads=[("ones",)] + [("oT", k, hg) for k in range(NCH)], writes=[("ps", B_BC)], cost=2.0)
            P.op("act", lambda e: e.activation(out=rb_[:, :], in_=ps[:, B_BC, 0:HG], func=AF.Sqrt, scale=1.0 / D, bias=EPS),
                 writes=[("ps", B_BC), ("rsd", hg % 2)], cost=0.45)
            P.op("dve", lambda e: e.reciprocal(out=rb_[:, :], in_=rb_[:, :]), reads=[("rsd", hg % 2)], writes=[("rsd", hg % 2)],
                 cost=0.4)

        def p1a_norm(k, hg, t0):
            xt = xt2[hg % 2]
            rb_ = rsd[hg % 2]
            P.op("dve", lambda e: e.scalar_tensor_tensor(out=hT[:, k, t0:t0 + HG], in0=xt[:, k, :], scalar=G1[:, k:k + 1],
                                                         in1=rb_[:, :], op0=ALU.mult, op1=ALU.mult),
                 reads=[("xt", hg % 2, k), ("G1",), ("rsd", hg % 2)], writes=[("hT", k, t0 // TG, (t0 % TG) // HG)], cost=0.42)

        def phase1a():
            for hg in range(S // HG):
                t0 = hg * HG
                for q in range(4):
                    p1a_load(q, hg, t0)
                for k in range(NCH):
                    p1a_square(k, hg, t0)
                p1a_rstd(hg, t0)
                for k in range(NCH):
                    p1a_norm(k, hg, t0)
                yield

        for _ in phase1a():
            pass

        out_dmas = []

        class WStream:
            def __init__(self, name, slots, loads, depth):
                self.name, self.slots, self.loads, self.depth = name, slots, loads, depth
                self.issued = 0

            post = None

            def _issue(self, i):
                slot = i % len(self.slots)
                dst = self.slots[slot]
                src = self.loads[i]
                n = src.shape[-1]
                name = self.name
                P.op("pool", lambda e: e.dma_start(out=dst[:, 0:n], in_=src), writes=[(name, slot)], dma=True, cost=2.0 + n * 128 * 4 / 330e3)
                if self.post is not None:
                    self.post(i, dst, (name, slot))

            def get(self, i):
                while self.issued < min(len(self.loads), i + self.depth + 1):
                    self._issue(self.issued)
                    self.issued += 1
                slot = i % len(self.slots)
                return self.slots[slot], (self.name, slot)

        def fold_adaln(blk, wslot, wkey):
            def mmb(e):
                for k in range(NCH):
                    ins = e.matmul(ps[:, B_ST, 8:9], lhsT=wslot[:, k * 128:(k + 1) * 128], rhs=sh1b[:, k:k + 1],
                                   start=(k == 0), stop=(k == NCH - 1), skip_group_check=True)
                return ins
            P.op("pe", mmb, reads=[wkey, ("sh1b",)], writes=[("ps", B_ST)], cost=1.2)
            P.op("dve", lambda e: e.tensor_copy(out=biasT[:, blk:blk + 1], in_=ps[:, B_ST, 8:9]),
                 writes=[("ps", B_ST), ("biasT", blk)], cost=0.15)


        def proj(wslot, wkey, tg, bank):
            t0 = tg * TG

            def mm(e):
                for k in range(NCH):
                    ins = e.matmul(bankv(bank), lhsT=wslot[:, k * 128:(k + 1) * 128], rhs=hT[:, k, t0:t0 + TG],
                                   start=(k == 0), stop=(k == NCH - 1))
                return ins
            P.op("pe", mm, reads=[wkey] + [("hT", k, tg, hh) for k in range(NCH) for hh in range(2)], writes=[("ps", bank)], cost=16 * 0.26)

        if upto >= 2:
            P.barrier()
            AR.release(m_r1b)
            NW1 = 3
            wslots1 = [AR.alloc(2048 * 2, BF16) for _ in range(NW1)]
            wsT = AR.alloc(1024 * 2, BF16, (8, 128))
            bspb = AR.alloc(1024 * 4, F32, (8, 128))
            u_sb = [AR.alloc(TG * 2, BF16) for _ in range(3)]
            v_sb = [AR.alloc(TG * 2, BF16) for _ in range(2)]
            vn = [AR.alloc(TG * 2, BF16, (4, 128)) for _ in range(2)]
            gtmp = [AR.alloc(TG * 4, F32)] * 2
            go = [AR.alloc(TG * 4, F32) for _ in range(3)]
            gsq = [AR.alloc(TG * 2, BF16) for _ in range(2)]
            bst = [AR.alloc(4 * 6 * 4, F32, (4, 6)) for _ in range(2)]
            bmv = [AR.alloc(4 * 2 * 4, F32, (4, 2)) for _ in range(2)]
            brs = [AR.alloc(4 * 4, F32) for _ in range(2)]
            bnb = [AR.alloc(4 * 4, F32) for _ in range(2)]

            P.op("pool", lambda e: e.dma_start(out=wsT.rearrange("p a b -> p (a b)"), in_=wsT_d), writes=[("wsT",)], dma=True)
            P.op("sp", lambda e: e.dma_start(out=bspb.rearrange("p a b -> p (a b)"), in_=bsp_d.partition_broadcast(128)),
                 writes=[("bspb",)], dma=True)

            def ws_mask(g):
                P.op("pool", lambda e: e.tensor_tensor(out=wsT[:, g, :], in0=wsT[:, g, :], in1=ltri, op=ALU.mult),
                     reads=[("wsT",), ("cm",)], writes=[("wsT",)])
            for g in range(8):
                ws_mask(g)

            ws1 = WStream("w1", wslots1, [w_in_r[i] for i in range(16)], depth=1)
            ws1.post = lambda i, dst, key: fold_adaln(i, dst, key)
            B_TB, B_MX = 4, 5
            units = [(g, tg) for g in range(8) for tg in range(NTG)]
            tbv1 = ps[:, B_TB, 0:256].bitcast(BF16)
            rslots = {}

            def S_PU(n):
                g, tg = units[n]
                wslot, wkey = ws1.get(2 * g)
                bu = n % 2
                ub = u_sb[n % 3]
                proj(wslot, wkey, tg, bu)
                P.op("act", lambda e: e.activation(out=ub[:, :], in_=bankv(bu), func=AF.Gelu, bias=biasT[:, 2 * g:2 * g + 1]),
                     reads=[("biasT", 2 * g)], writes=[("ps", bu), ("u_sb", n % 3)])

            def S_PV(n):
                g, tg = units[n]
                wslot, wkey = ws1.get(2 * g + 1)
                bv = 2 + n % 2
                vb = v_sb[n % 2]
                proj(wslot, wkey, tg, bv)
                P.op("act", lambda e: e.activation(out=vb[:, :], in_=bankv(bv), func=AF.Gelu, bias=biasT[:, 2 * g + 1:2 * g + 2]),
                     reads=[("biasT", 2 * g + 1)], writes=[("ps", bv), ("v_sb", n % 2)])

            def S_TR(n):
                par = n % 2
                vb, st6, mv, rs = v_sb[par], bst[par], bmv[par], brs[par]

                def tr(e):
                    for i in range(4):
                        ins = e.transpose(tbv1[:, i * 128:(i + 1) * 128], vb[:, i * 128:(i + 1) * 128], ident)
                    return ins
                P.op("pe", tr, reads=[("v_sb", par), ("cm",)], writes=[("ps", B_TB)], cost=0.45)

                def stats(e):
                    for i in range(4):
                        ins = e.bn_stats(st6[:, i, :], tbv1[:, i * 128:(i + 1) * 128])
                    return ins
                P.op("dve", stats, writes=[("ps", B_TB), ("bst", par)], cost=0.9)

                def aggr(e):
                    for i in range(4):
                        ins = e.bn_aggr(mv[:, i, :], st6[:, i, :])
                    return ins
                P.op("dve", aggr, reads=[("bst", par)], writes=[("bmv", par)], cost=0.4)
                P.op("dve", lambda e: e.tensor_scalar(out=rs[:, :], in0=mv[:, :, 1], scalar1=EPS, scalar2=None, op0=ALU.add),
                     reads=[("bmv", par)], writes=[("brs", par)])
                P.op("pool", lambda e: e.tensor_tensor(out=rs[:, :], in0=rs[:, :], in1=mhalf[:, 0:4], op=ALU.pow),
                     reads=[("mhalf",)], writes=[("brs", par)], cost=1.1)

            def S_VN(n):
                g, tg = units[n]
                par = n % 2
                vnb, mv, rs = vn[par], bmv[par], brs[par]

                nb = bnb[par]
                P.op("dve", lambda e: e.scalar_tensor_tensor(out=nb[:, :], in0=mv[:, :, 0], scalar=-1.0, in1=rs[:, :],
                                                             op0=ALU.mult, op1=ALU.mult),
                     reads=[("brs", par), ("bmv", par)], writes=[("bnb", par)])

                def vnorm(e):
                    for i in range(4):
                        ins = e.activation(out=vnb[:, i, :], in_=tbv1[:, i * 128:(i + 1) * 128], func=AF.Identity,
                                           scale=rs[:, i:i + 1], bias=nb[:, i:i + 1])
                    return ins
                P.op("act", vnorm, reads=[("brs", par), ("bnb", par)], writes=[("ps", B_TB), ("vn", par)], cost=1.5)

                def mix(e):
                    for i in range(4):
                        ins = e.matmul(ps[:, B_MX, i * 128:(i + 1) * 128], lhsT=vnb[:, i, :], rhs=wsT[:, g, :],
                                       start=True, stop=True, skip_group_check=True)
                    return ins
                P.op("pe", mix, reads=[("vn", par), ("wsT",)], writes=[("ps", B_MX)], cost=0.5)

            def S_EP(n):
                g, tg = units[n]
                par = n % 2
                tb, ob, sqb, ub = gtmp[par], go[n % 3], gsq[par], u_sb[n % 3]

                def ep1(e):
                    for i in range(4):
                        ins = e.scalar_tensor_tensor(out=tb[:, i * 128:(i + 1) * 128], in0=ps[:, B_MX, i * 128:(i + 1) * 128],
                                                     scalar=vecs[:, V_GV + g:V_GV + g + 1], in1=bspb[:, g, :],
                                                     op0=ALU.mult, op1=ALU.add)
                    return ins
                P.op("dve", ep1, reads=[("vecs",), ("bspb",)], writes=[("ps", B_MX), ("gtmp", 0)], cost=1.2)
                P.op("dve", lambda e: e.tensor_tensor(out=ob[:, :], in0=tb[:, :], in1=ub[:, :], op=ALU.mult),
                     reads=[("gtmp", 0), ("u_sb", n % 3)], writes=[("go", n % 3)], cost=0.7)
                P.op("act", lambda e: e.activation(out=sqb[:, :], in_=ob[:, :], func=AF.Square),
                     reads=[("go", n % 3)], writes=[("gsq", par)])

            def S_STAT(n):
                par = n % 2
                rslots[n] = rstd_p1([gsq[par]], [("gsq", par)], 128.0)

            def S_BC(n):
                rstd_p1b(rslots[n])
                rstd_p2(rslots[n])

            def S_FIN(n):
                g, tg = units[n]
                t0 = tg * TG
                ob = go[n % 3]
                P.op("dve", lambda e: e.scalar_tensor_tensor(
                    out=oT[:, g, t0:t0 + TG], in0=ob[:, :], scalar=vecs[:, V_GON + g:V_GON + g + 1], in1=bankv(B_BC),
                    op0=ALU.mult, op1=ALU.mult),
                    reads=[("go", n % 3), ("vecs",)], writes=[("ps", B_BC), ("oT", g, tg)])

            def phase1b():
                N = len(units)
                stages = [S_PU, S_PV, S_TR, S_VN, S_EP, S_STAT, S_BC, S_FIN]
                for hs in range(2 * N + 8):
                    for k, fn in enumerate(stages):
                        d = hs - k
                        if d >= 0 and d % 2 == 0 and d // 2 < N:
                            fn(d // 2)
                    yield

            roundrobin([phase1b(), mod2_thread()], [1, 1])

        if upto >= 3:
            P.barrier()
            AR.release(m_r1)
            NW2 = 3
            wslots2 = [AR.alloc(2048 * 2, BF16) for _ in range(NW2)]
            qT = [AR.alloc(S * 2, BF16) for _ in range(2)]
            kT = [AR.alloc(S * 2, BF16) for _ in range(2)]
            vtok = [AR.alloc(S * 2, BF16, (16, 128)) for _ in range(2)]
            vst = [AR.alloc(TG * 2, BF16) for _ in range(2)]
            NE, NSP, NPM, NTT, NAT = 2, 4, 4, 2, 3
            e_b = [AR.alloc(TG * 4, F32) for _ in range(NE)]
            sp_b = [AR.alloc(TG * 4, F32) for _ in range(NSP)]
            pm_b = [AR.alloc(TG * 2, BF16) for _ in range(NPM)]
            tt_b = [AR.alloc(TG * 4, F32) for _ in range(NTT)]
            at_b = [AR.alloc(TG * 2, BF16) for _ in range(NAT)]
            osb = [AR.alloc(TG * 4, F32) for _ in range(3)]
            asq = [AR.alloc(TG * 2, BF16) for _ in range(3)]
            ws2 = WStream("w2", wslots2, [w_in_r[16 + i] for i in range(24)], depth=0)
            ws2.post = lambda i, dst, key: fold_adaln(16 + i, dst, key)
            B_Z, B_ACC, B_AV, B_PJ, B_VT = (0, 1), 2, 3, (4, 5), 6

            pj_cnt = [0]
            NE, NSP, NPM, NTT, NAT = 2, 4, 4, 2, 3

            def proj_part(wslot, wkey, tg, bank, k0, k1):
                t0 = tg * TG

                def mm(e):
                    for k in range(k0, k1):
                        ins = e.matmul(bankv(bank), lhsT=wslot[:, k * 128:(k + 1) * 128], rhs=hT[:, k, t0:t0 + TG],
                                       start=(k == 0), stop=(k == NCH - 1))
                    return ins
                P.op("pe", mm, reads=[wkey] + [("hT", k, tg, hh) for k in range(k0, k1) for hh in range(2)], writes=[("ps", bank)], cost=(k1 - k0) * 0.26)

            def qkv_evac(h, which, tg, bank):
                hp = h % 2
                t0 = tg * TG
                blk = 16 + 3 * h + which
                bcol = biasT[:, blk:blk + 1]
                if which == 0:
                    P.op("dve", lambda e: e.tensor_scalar(out=qT[hp][:, t0:t0 + TG], in0=bankv(bank), scalar1=bcol, scalar2=None, op0=ALU.add),
                         reads=[("biasT", blk)], writes=[("ps", bank), ("qT", hp, tg)])
                elif which == 1:
                    P.op("dve", lambda e: e.tensor_scalar(out=kT[hp][:, t0:t0 + TG], in0=bankv(bank), scalar1=bcol, scalar2=None, op0=ALU.add),
                         reads=[("biasT", blk)], writes=[("ps", bank), ("kT", hp, tg)])
                else:
                    sp_ = tg % 2
                    vs = vst[sp_]
                    P.op("dve", lambda e: e.tensor_scalar(out=vs[:, :], in0=bankv(bank), scalar1=bcol, scalar2=None, op0=ALU.add),
                         reads=[("biasT", blk)], writes=[("ps", bank), ("vst", sp_)])

            def v_transpose(h, tg):
                hp = h % 2
                sp_ = tg % 2
                vs = vst[sp_]
                tbv = ps[:, B_VT, 256:512].bitcast(BF16)

                def tr(e):
                    for i in range(4):
                        ins = e.transpose(tbv[:, i * 128:(i + 1) * 128], vs[:, i * 128:(i + 1) * 128], ident)
                    return ins
                P.op("pe", tr, reads=[("vst", sp_), ("cm",)], writes=[("ps", B_VT)], cost=0.45)

            def v_evac(h, tg):
                hp = h % 2
                tbv = ps[:, B_VT, 256:512].bitcast(BF16)
                P.op("dve", lambda e: e.tensor_copy(out=vtok[hp][:, 4 * tg:4 * tg + 4, :],
                                                    in_=tbv.rearrange("p (a b) -> p a b", a=4)),
                     writes=[("ps", B_VT), ("vtok", hp, tg)])

            def qkv_thread(h):
                for tg in range(NTG):
                    for which in range(3):
                        wslot, wkey = ws2.get(3 * h + which)
                        bank = B_PJ[pj_cnt[0] % 2]
                        pj_cnt[0] += 1
                        proj_part(wslot, wkey, tg, bank, 0, 8)
                        yield
                        proj_part(wslot, wkey, tg, bank, 8, 16)
                        yield
                        qkv_evac(h, which, tg, bank)
                        if which == 2:
                            yield
                            v_transpose(h, tg)
                            yield
                            v_evac(h, tg)
                        yield

            steps = []
            for h in range(8):
                for tg in range(NTG):
                    nb = 4 * tg + 4
                    for j, sbk in enumerate(range(nb - 1, -1, -1)):
                        i_d = sbk - 4 * tg
                        c0 = max(i_d, 0) * 128
                        steps.append(dict(h=h, tg=tg, sb=sbk, c0=c0, diag=(i_d >= 0), first=(j == 0), last=(sbk == 0)))

            def st_A1(n, s):
                h, tg, sbk, c0 = s["h"], s["tg"], s["sb"], s["c0"]
                hp = h % 2
                t0 = tg * TG
                zb = B_Z[n % 2]
                eb, spb = e_b[n % NE], sp_b[n % NSP]
                P.op("pe", lambda e: e.matmul(ps[:, zb, c0:TG], lhsT=kT[hp][:, sbk * 128:(sbk + 1) * 128],
                                              rhs=qT[hp][:, t0 + c0:t0 + TG], start=True, stop=True),
                     reads=[("kT", hp, sbk // 4), ("qT", hp, tg)], writes=[("ps", zb)], cost=0.3)
                P.op("act", lambda e: e.activation(out=eb[:, c0:TG], in_=ps[:, zb, c0:TG], func=AF.Exp, scale=-SCALE),
                     writes=[("ps", zb), ("e", n % NE)])
                P.op("act", lambda e: e.activation(out=spb[:, c0:TG], in_=eb[:, c0:TG], func=AF.Ln, bias=1.0),
                     reads=[("e", n % NE)], writes=[("sp", n % NSP)])

            def st_A2(n, s):
                c0 = s["c0"]
                zb = B_Z[n % 2]
                spb, pmb = sp_b[n % NSP], pm_b[n % NPM]
                P.op("dve", lambda e: e.scalar_tensor_tensor(out=pmb[:, c0:TG], in0=ps[:, zb, c0:TG], scalar=SCALE,
                                                             in1=spb[:, c0:TG], op0=ALU.mult, op1=ALU.add),
                     reads=[("sp", n % NSP)], writes=[("ps", zb), ("pm", n % NPM)])
                if s["diag"]:
                    P.op("dve", lambda e: e.tensor_tensor(out=pmb[:, c0:c0 + 128], in0=pmb[:, c0:c0 + 128], in1=maskd, op=ALU.mult),
                         reads=[("cm",), ("pm", n % NPM)], writes=[("pm", n % NPM)])

            def st_B(n, s):
                c0 = s["c0"]
                spb, pmb, ttb, atb = sp_b[n % NSP], pm_b[n % NPM], tt_b[n % NTT], at_b[n % NAT]
                first = s["first"]
                pc0 = steps[n - 1]["c0"] if not first else 0
                ppm = pm_b[(n - 1) % NPM]

                def mm(e):
                    if not first:
                        e.matmul(ps[:, B_ACC, pc0:TG], lhsT=ltri, rhs=ppm[:, pc0:TG], start=False, stop=False,
                                 skip_group_check=True)
                    return e.matmul(ps[:, B_ACC, c0:TG], lhsT=tri, rhs=pmb[:, c0:TG], start=first, stop=s["last"],
                                    skip_group_check=True)
                rd = [("pm", n % NPM), ("cm",)]
                if not first:
                    rd.append(("pm", (n - 1) % NPM))
                P.op("pe", mm, reads=rd, writes=[("ps", B_ACC)], cost=0.55)
                P.op("dve", lambda e: e.tensor_tensor(out=ttb[:, c0:TG], in0=ps[:, B_ACC, c0:TG], in1=spb[:, c0:TG], op=ALU.add),
                     reads=[("sp", n % NSP)], writes=[("ps", B_ACC), ("tt", n % NTT)])

            def st_B2(n, s):
                c0 = s["c0"]
                ttb, atb = tt_b[n % NTT], at_b[n % NAT]
                P.op("act", lambda e: e.activation(out=atb[:, c0:TG], in_=ttb[:, c0:TG], func=AF.Exp, scale=-1.0),
                     reads=[("tt", n % NTT)], writes=[("at", n % NAT)])

            def st_B3(n, s):
                c0 = s["c0"]
                atb = at_b[n % NAT]
                if s["diag"]:
                    P.op("dve", lambda e: e.tensor_tensor(out=atb[:, c0:c0 + 128], in0=atb[:, c0:c0 + 128], in1=maskd, op=ALU.mult),
                         reads=[("cm",), ("at", n % NAT)], writes=[("at", n % NAT)])

            ch_cnt = [0]
            deferred = []

            def chain_end1(par):
                ob, sqb = osb[par], asq[par]
                P.op("dve", lambda e: e.tensor_copy(out=ob[:, :], in_=bankv(B_AV)), writes=[("ps", B_AV), ("osb", par)])
                P.op("act", lambda e: e.activation(out=sqb[:, :], in_=ob[:, :], func=AF.Square),
                     reads=[("osb", par)], writes=[("asq", par)])

            def chain_end5(h, tg, par):
                t0 = tg * TG
                ob = osb[par]
                P.op("dve", lambda e: e.scalar_tensor_tensor(
                    out=oT[:, 8 + h, t0:t0 + TG], in0=ob[:, :], scalar=vecs[:, V_GON + 8 + h:V_GON + 9 + h],
                    in1=bankv(B_BC), op0=ALU.mult, op1=ALU.mult),
                    reads=[("osb", par), ("vecs",)], writes=[("ps", B_BC), ("oT", 8 + h, tg)])

            def st_D(n, s, it):
                h, tg, sbk, c0 = s["h"], s["tg"], s["sb"], s["c0"]
                hp = h % 2
                atb = at_b[n % NAT]
                P.op("pe", lambda e: e.matmul(ps[:, B_AV, c0:TG], lhsT=vtok[hp][:, sbk, :], rhs=atb[:, c0:TG],
                                              start=s["first"], stop=s["last"], skip_group_check=True),
                     reads=[("at", n % NAT), ("vtok", hp, sbk // 4)], writes=[("ps", B_AV)], cost=0.28)
                if s["last"]:
                    par = ch_cnt[0] % 3
                    ch_cnt[0] += 1
                    box = {}

                    def f1(par=par):
                        chain_end1(par)

                    def f2(par=par, box=box):
                        box["slot"] = rstd_p1([asq[par]], [("asq", par)], 128.0)

                    def f3(box=box):
                        rstd_p1b(box["slot"])

                    def f4(box=box):
                        rstd_p2(box["slot"])

                    def f5(h=h, tg=tg, par=par):
                        chain_end5(h, tg, par)
                    f1()
                    for dly, f in ((1, f2), (2, f3), (3, f4), (4, f5)):
                        deferred.append((it + dly, f))

            def attn_thread():
                N = len(steps)
                it = 0
                while True:
                    busy = False
                    if 0 <= it - 3 < N:
                        st_B(it - 3, steps[it - 3]); busy = True
                    if it < N:
                        st_A1(it, steps[it]); busy = True
                    if 0 <= it - 1 < N:
                        st_A2(it - 1, steps[it - 1]); busy = True
                    if 0 <= it - 3 < N:
                        st_B2(it - 3, steps[it - 3])
                    if 0 <= it - 4 < N:
                        st_B3(it - 4, steps[it - 4])
                        st_D(it - 4, steps[it - 4], it); busy = True
                    for d in [d for d in deferred if d[0] <= it]:
                        d[1]()
                        deferred.remove(d)
                        busy = True
                    if deferred:
                        busy = True
                    if not busy:
                        break
                    it += 1
                    yield

            for _ in qkv_thread(0):
                pass
            ag = attn_thread()
            per_head = len(steps) // 8
            for h in range(8):
                qg = qkv_thread(h + 1) if h + 1 < 8 else None
                for i in range(per_head):
                    next(ag)
                    if qg is not None:
                        next(qg, None)
                if qg is not None:
                    for _ in qg:
                        pass
            for _ in ag:
                pass

        if upto >= 4:
            P.barrier()
            AR.release(0)
            x1 = AR.alloc(NCH * TG * 4, F32, (NCH, TG))
            h2T = AR.alloc(NCH * TG * 2, BF16, (NCH, TG))
            actT = AR.alloc(NFF * TG * 2, BF16, (NFF, TG))
            NW3 = 3
            wslots3 = [AR.alloc(11 * 512 * 2, BF16) for _ in range(NW3)]
            s_sb = [AR.alloc(TG * 2, BF16) for _ in range(2)]
            tmp3 = [AR.alloc(TG * 4, F32) for _ in range(2)]
            ostg = [AR.alloc(TG * 4, F32) for _ in range(2)]
            loads = []
            for st in range(NTG):
                loads += [w_out_r[i] for i in range(8)]
                for cg in range(22):
                    loads += [w_gate_r[cg], w_up_r[cg]]
                loads += [w_down_r[i] for i in range(16)]
            ws3 = WStream("w3", wslots3, loads, depth=1)
            wi = [0]
            rr = [0]

            def nb3():
                b = rr[0]
                rr[0] = (b + 1) % 6
                return b

            def mm_piece(banks, wslot, wkey, ncb, kc, rhs_of, rhs_keys, k0, ktot, split=False):
                rhs_list = [rhs_of(k0 + k) for k in range(kc)]
                wc = ncb * 128

                def mm(e):
                    for cb_ in range(ncb):
                        for k in range(kc):
                            kk = k0 + k
                            ins = e.matmul(bankv(banks[cb_]), lhsT=wslot[:, k * wc + cb_ * 128:k * wc + (cb_ + 1) * 128],
                                           rhs=rhs_list[k], start=(kk == 0), stop=(kk == ktot - 1), skip_group_check=True)
                    return ins
                if not split:
                    def mcb(cb_):
                        def mmc(e):
                            for k in range(kc):
                                kk = k0 + k
                                ins = e.matmul(bankv(banks[cb_]), lhsT=wslot[:, k * wc + cb_ * 128:k * wc + (cb_ + 1) * 128],
                                               rhs=rhs_list[k], start=(kk == 0), stop=(kk == ktot - 1), skip_group_check=True)
                            return ins
                        P.op("pe", mmc, reads=[wkey] + list(rhs_keys), writes=[("ps", banks[cb_])], cost=kc * 0.26)
                    for cb_ in range(ncb):
                        mcb(cb_)
                    return

                def mk(k):
                    kk = k0 + k

                    def mmk(e):
                        for cb_ in range(ncb):
                            ins = e.matmul(bankv(banks[cb_]), lhsT=wslot[:, k * wc + cb_ * 128:k * wc + (cb_ + 1) * 128],
                                           rhs=rhs_list[k], start=(kk == 0), stop=(kk == ktot - 1), skip_group_check=True)
                        return ins
                    P.op("pe", mmk, reads=[wkey, rhs_keys[k]], writes=[("ps", b) for b in banks], cost=ncb * 0.26)
                for k in range(kc):
                    mk(k)

            def rstd3(keys):
                def mm(e):
                    for k in range(NCH):
                        ins = e.matmul(bankv(B_BC), lhsT=ones[:, :], rhs=actT[:, k, :], start=(k == 0), stop=(k == NCH - 1))
                    return ins
                P.op("pe", mm, reads=[("ones",)] + list(keys), writes=[("ps", B_BC)], cost=3.6)
                P.op("act", lambda e: e.activation(out=bankv(B_BC), in_=bankv(B_BC), func=AF.Sqrt, scale=1.0 / D, bias=EPS),
                     writes=[("ps", B_BC)], cost=0.7)
                P.op("dve", lambda e: e.reciprocal(out=bankv(B_BC), in_=bankv(B_BC)), writes=[("ps", B_BC)], cost=0.7)

            def ld_x(q, t0):
                for k in range(4 * q, 4 * q + 4):
                    ld_x1(k, t0)

            def ld_x1(k, t0):
                P.op("sp", lambda e: e.dma_start(out=x1[:, k, :], in_=xT_v[:, k, t0:t0 + TG]),
                     writes=[("x1", k)], dma=True, cost=3.0)

            def resid(bank, db, gcol):
                P.op("dve", lambda e: e.scalar_tensor_tensor(out=x1[:, db, :], in0=bankv(bank), scalar=gcol,
                                                             in1=x1[:, db, :], op0=ALU.mult, op1=ALU.add),
                     reads=[("modT", 2), ("modT", 5)], writes=[("ps", bank), ("x1", db)])

            def sq_x(db):
                P.op("act", lambda e: e.activation(out=actT[:, db, :], in_=x1[:, db, :], func=AF.Square),
                     reads=[("x1", db)], writes=[("actT", db)])

            def norm2(db):
                tb = tmp3[db % 2]
                P.op("dve", lambda e: e.tensor_tensor(out=tb[:, :], in0=x1[:, db, :], in1=bankv(B_BC), op=ALU.mult),
                     reads=[("x1", db)], writes=[("tmp3", db % 2), ("ps", B_BC)])
                P.op("act", lambda e: e.activation(out=h2T[:, db, :], in_=tb[:, :], func=AF.Identity,
                                                   scale=G2[:, db:db + 1], bias=SH2[:, db:db + 1]),
                     reads=[("tmp3", db % 2), ("G2",), ("modT", 3)], writes=[("h2T", db)])

            def swiglu(gb, ub, fb):
                sb_ = s_sb[fb % 2]
                P.op("act", lambda e: e.activation(out=sb_[:, :], in_=bankv(gb), func=AF.Silu),
                     writes=[("ps", gb), ("s_sb", fb % 2)])
                P.op("dve", lambda e: e.tensor_tensor(out=actT[:, fb, :], in0=bankv(ub), in1=sb_[:, :], op=ALU.mult),
                     reads=[("s_sb", fb % 2)], writes=[("ps", ub), ("actT", fb)])

            def final(db, t0):
                j = db % 4
                ob = tmp3[j] if j < 2 else ostg[j - 2]
                key = ("tmp3", j) if j < 2 else ("ostg", j - 2)
                P.op("dve", lambda e: e.scalar_tensor_tensor(out=ob[:, :], in0=x1[:, db, :],
                                                             scalar=vecs[:, V_GF + db:V_GF + db + 1], in1=bankv(B_BC),
                                                             op0=ALU.mult, op1=ALU.mult),
                     reads=[("vecs",), ("x1", db)], writes=[key, ("ps", B_BC)])
                out_dmas.append(P.op("sp", lambda e: e.dma_start(out=outT_v[:, db, t0:t0 + TG], in_=ob[:, :]),
                                     reads=[key], dma=True, cost=3.0))

            def st_out(q, t0):
                pass

            def phase3():
                for st in range(NTG):
                    t0 = st * TG
                    for q in range(4):
                        ld_x(q, t0)
                    for ng in range(4):
                        banks = [nb3() for _ in range(4)]
                        for kp in range(2):
                            wslot, wkey = ws3.get(wi[0]); wi[0] += 1
                            mm_piece(banks, wslot, wkey, 4, 8, lambda kk: oT[:, kk, t0:t0 + TG],
                                     [("oT", kk, st) for kk in range(8 * kp, 8 * kp + 8)], 8 * kp, 16)
                        for cb_ in range(4):
                            db = ng * 4 + cb_
                            resid(banks[cb_], db, GT1[:, db:db + 1])
                            sq_x(db)
                        yield
                    rstd3([("actT", db) for db in range(NCH)])
                    for db in range(NCH):
                        norm2(db)
                    yield
                    for cg in range(22):
                        gbanks = [nb3() for _ in range(2)]
                        wslot, wkey = ws3.get(wi[0]); wi[0] += 1
                        mm_piece(gbanks, wslot, wkey, 2, 16, lambda kk: h2T[:, kk, :], [("h2T", kk) for kk in range(16)], 0, 16,
                                 split=(cg == 0))
                        ubanks = [nb3() for _ in range(2)]
                        wslot, wkey = ws3.get(wi[0]); wi[0] += 1
                        mm_piece(ubanks, wslot, wkey, 2, 16, lambda kk: h2T[:, kk, :], [("h2T", kk) for kk in range(16)], 0, 16,
                                 split=(cg == 0))
                        for j in range(2):
                            swiglu(gbanks[j], ubanks[j], cg * 2 + j)
                        yield
                    for ng in range(4):
                        banks = [nb3() for _ in range(4)]
                        for kp in range(4):
                            wslot, wkey = ws3.get(wi[0]); wi[0] += 1
                            mm_piece(banks, wslot, wkey, 4, 11, lambda kk: actT[:, kk, :],
                                     [("actT", kk) for kk in range(11 * kp, 11 * kp + 11)], 11 * kp, 44)
                        for cb_ in range(4):
                            db = ng * 4 + cb_
                            resid(banks[cb_], db, GT2[:, db:db + 1])
                        yield
                    for db in range(NCH):
                        sq_x(db)
                    rstd3([("actT", db) for db in range(NCH)])
                    for db in range(NCH):
                        final(db, t0)
                    for q in range(4):
                        st_out(q, t0)
                    yield

            for _ in phase3():
                pass

        if debug:
            P.barrier()
            if upto < 4:
                out_dmas.append(P.op("pool", lambda e: e.dma_start(out=dbg["hT"], in_=hT.rearrange("p a b -> p (a b)")),
                                     reads=[], dma=True))
            out_dmas.append(P.op("pool", lambda e: e.dma_start(out=dbg["oT"], in_=oT[:, :, :].rearrange("p a b -> p (a b)")),
                                 reads=[], dma=True))
            out_dmas.append(P.op("sp", lambda e: e.dma_start(out=dbg["mod"], in_=modT[:, :]), reads=[], dma=True))

        est = P.schedule(reorder=REORDER)
        print("arena high-water", AR.hi, "ops", {e: len(v) for e, v in P.streams.items()}, "sched est us", round(est))
        with nc.Block() as block:
            @block.tensor
            def _(e):
                P.replay("pe", e)

            @block.scalar
            def _(e):
                P.replay("act", e)

            @block.vector
            def _(e):
                P.replay("dve", e)

            @block.gpsimd
            def _(e):
                w = P.replay("pool", e)
                P.final_wait("pool", e, w, [o for o in out_dmas if o.eng == "pool"])

            @block.sync
            def _(e):
                w = P.replay("sp", e)
                P.final_wait("sp", e, w, out_dmas)
    return nc


def _col_layout(v, n):
    return np.ascontiguousarray(np.asarray(v, np.float32).reshape(n, 128).T)


def _tile_w(w, ncols, kc):
    K, N = w.shape
    nk = K // 128
    a = w.reshape(nk // kc, kc, 128, N // ncols, ncols)
    a = a.transpose(3, 0, 2, 1, 4)
    return np.ascontiguousarray(a.reshape((N // ncols) * (nk // kc), 128, kc * ncols))


def prepare_inputs(x, c, w_ada, b_ada, norm1_g, w_in, v_norm_g, w_spatial, b_spatial,
                   out_norm_g, w_out, norm2_g, w_gate, w_up, w_down, final_g):
    f = lambda a: np.asarray(a, np.float32)
    x, c = f(x), f(c)
    wadaT = np.ascontiguousarray(f(w_ada)[0].T)
    vecs = np.zeros((128, NVEC), np.float32)
    vecs[:, V_BADA:V_BADA + 96] = _col_layout(f(b_ada)[0], 96)
    vecs[:, V_G1:V_G1 + 16] = _col_layout(f(norm1_g)[0], 16)
    vecs[:, V_G2:V_G2 + 16] = _col_layout(f(norm2_g)[0], 16)
    vecs[:, V_GF:V_GF + 16] = _col_layout(f(final_g), 16)
    vecs[:, V_GON:V_GON + 16] = _col_layout(f(out_norm_g)[0], 16)
    vecs[:, V_GV:V_GV + 8] = _col_layout(f(v_norm_g)[0], 8)
    bsp = np.ascontiguousarray(f(b_spatial)[0].reshape(1, 1024))
    wsT = np.ascontiguousarray(f(w_spatial)[0].transpose(2, 0, 1).reshape(128, 1024))
    p = np.arange(128)[:, None]
    q = np.arange(128)[None, :]
    cmask = np.concatenate([(p > q), (p <= q), (q > p), (p == q)], axis=1).astype(np.float32)
    w_in0 = f(w_in)[0]
    cols = []
    for g in range(8):
        cols += [g * 128, 1024 + g * 128]
    for h in range(8):
        cols += [2048 + h * 128, 3072 + h * 128, 4096 + h * 128]
    w_in_r = np.empty((40, 128, 2048), np.float32)
    for i, c0 in enumerate(cols):
        blk = w_in0[:, c0:c0 + 128].reshape(16, 128, 128)
        w_in_r[i] = blk.transpose(1, 0, 2).reshape(128, 2048)
    shared = dict(
        wadaT=wadaT, vecs=vecs, bsp=bsp, wsT=wsT, cmask=cmask, w_in_r=w_in_r,
        w_out_r=_tile_w(f(w_out)[0], 512, 8),
        w_gate_r=_tile_w(f(w_gate)[0], 256, 16),
        w_up_r=_tile_w(f(w_up)[0], 256, 16),
        w_down_r=_tile_w(f(w_down)[0], 512, 11),
    )
    in_maps = []
    for b in range(x.shape[0]):
        m = dict(shared)
        m["xT"] = np.ascontiguousarray(x[b].T)
        m["cb"] = np.ascontiguousarray(c[b:b + 1])
        in_maps.append(m)
    return in_maps


def kernel(**inputs):
    in_maps = prepare_inputs(**inputs)
    nc = build_nc()
    res = run_bass_kernel_spmd(nc, in_maps, core_ids=list(range(len(in_maps))))
    out = np.stack([np.ascontiguousarray(r["outT"].T) for r in res.results], axis=0)
    return out.astype(np.float32)
```
